# Optimizing a Trainium2 kernel written in Bass

```python
import jax, jax.numpy as jnp
from jax import lax
import numpy as np

D_MODEL = 1024
BATCH = 2
SEQ = 16384
DEPTH = 2
DEC_BATCH = 16
DEC_SEQ = 4096
PAST_LEN = 128

N_MEM = 256
EPS = 1e-6
MLA_HEADS = 8
QK_NOPE = 64
QK_ROPE = 32
V_HEAD = 64
Q_RANK = 384
KV_RANK = 256
ROPE_THETA = 10000.0
Q_BLOCK = 128
FNET_GROUPS = 4
FNET_CH = 128
IN_W = Q_RANK + KV_RANK + QK_ROPE + FNET_GROUPS * FNET_CH
MIX_W = MLA_HEADS * V_HEAD + FNET_GROUPS * FNET_CH
CONV_K = 31
XA_HEADS = 4
XA_HEAD_DIM = D_MODEL // XA_HEADS
D_FF = -(-(8 * D_MODEL) // (3 * 256)) * 256
N_EVEN = (DEPTH + 1) // 2
N_ODD = DEPTH // 2

kernel_name = "hybrid_mla_fnet_conformer_encoder"


def _rmsnorm(x, g):
    xf = x.astype(jnp.float32)
    y = xf * lax.rsqrt(jnp.mean(xf * xf, axis=-1, keepdims=True) + EPS)
    return (y * g.astype(jnp.float32)).astype(x.dtype)


def _layernorm(x, g, b):
    xf = x.astype(jnp.float32)
    mu = jnp.mean(xf, axis=-1, keepdims=True)
    xc = xf - mu
    y = xc * lax.rsqrt(jnp.mean(xc * xc, axis=-1, keepdims=True) + EPS)
    return (y * g.astype(jnp.float32) + b.astype(jnp.float32)).astype(x.dtype)


def _rope(x):
    s = x.shape[1]
    half = QK_ROPE // 2
    inv = ROPE_THETA ** (-jnp.arange(half, dtype=jnp.float32) / half)
    ang = jnp.arange(s, dtype=jnp.float32)[:, None] * inv[None, :]
    shape = (1, s) + (1,) * (x.ndim - 3) + (half,)
    cos = jnp.cos(ang).reshape(shape)
    sin = jnp.sin(ang).reshape(shape)
    xf = x.astype(jnp.float32)
    x1, x2 = xf[..., :half], xf[..., half:]
    out = jnp.concatenate([x1 * cos - x2 * sin, x1 * sin + x2 * cos], axis=-1)
    return out.astype(x.dtype)


def _mla_attention(q_nope, q_rope, k_nope, k_rope, v):
    b, s, h, _ = q_nope.shape
    nb = s // Q_BLOCK
    scale = (QK_NOPE + QK_ROPE) ** -0.5
    qn = q_nope.reshape(b, nb, Q_BLOCK, h, QK_NOPE).transpose(1, 0, 2, 3, 4)
    qr = q_rope.reshape(b, nb, Q_BLOCK, h, QK_ROPE).transpose(1, 0, 2, 3, 4)

    def one_block(args):
        qn_blk, qr_blk = args
        sc = (jnp.einsum('bqhd,bkhd->bhqk', qn_blk, k_nope, preferred_element_type=jnp.float32)
              + jnp.einsum('bqhr,bkr->bhqk', qr_blk, k_rope, preferred_element_type=jnp.float32))
        p = jax.nn.softmax(sc * scale, axis=-1).astype(v.dtype)
        return jnp.einsum('bhqk,bkhd->bqhd', p, v)

    out = lax.map(one_block, (qn, qr))
    return out.transpose(1, 0, 2, 3, 4).reshape(b, s, h * V_HEAD)


def _mla_fnet_mixer(h, w_in, g_q, w_uq, g_kv, w_ukv, w_out):
    b, s, _ = h.shape
    z = h @ w_in
    c_q, c_kv, k_r, f = jnp.split(
        z, [Q_RANK, Q_RANK + KV_RANK, Q_RANK + KV_RANK + QK_ROPE], axis=-1)
    q = (_rmsnorm(c_q, g_q) @ w_uq).reshape(b, s, MLA_HEADS, QK_NOPE + QK_ROPE)
    q_nope, q_rope = q[..., :QK_NOPE], _rope(q[..., QK_NOPE:])
    kv = (_rmsnorm(c_kv, g_kv) @ w_ukv).reshape(b, s, MLA_HEADS, QK_NOPE + V_HEAD)
    k_nope, v = kv[..., :QK_NOPE], kv[..., QK_NOPE:]
    k_rope = _rope(k_r)
    attn = _mla_attention(q_nope, q_rope, k_nope, k_rope, v)
    fg = f.reshape(b, s, FNET_GROUPS, FNET_CH).astype(jnp.float32)
    fo = jnp.real(jnp.fft.fft2(fg, axes=(1, 3), norm='ortho'))
    fo = fo.astype(h.dtype).reshape(b, s, FNET_GROUPS * FNET_CH)
    return jnp.concatenate([attn, fo], axis=-1) @ w_out


def _conformer_conv(h, w_pw1, b_pw1, w_dw, b_dw, g_ln, b_ln, w_pw2, b_pw2):
    u = h @ w_pw1 + b_pw1
    a, gate = jnp.split(u, 2, axis=-1)
    u = a * jax.nn.sigmoid(gate)
    u = lax.conv_general_dilated(
        u, w_dw[:, None, :].astype(u.dtype), window_strides=(1,),
        padding=[(CONV_K // 2, CONV_K // 2)],
        dimension_numbers=('NWC', 'WIO', 'NWC'),
        feature_group_count=D_MODEL) + b_dw
    u = jax.nn.silu(_layernorm(u, g_ln, b_ln))
    return u @ w_pw2 + b_pw2


def _mem_xattn(h, m, wq, wk, wv, wo):
    b, s, _ = h.shape
    nm = m.shape[1]
    q = (h @ wq).reshape(b, s, XA_HEADS, XA_HEAD_DIM)
    k = (m @ wk).reshape(b, nm, XA_HEADS, XA_HEAD_DIM)
    v = (m @ wv).reshape(b, nm, XA_HEADS, XA_HEAD_DIM)
    sc = jnp.einsum('bqhd,bkhd->bhqk', q, k, preferred_element_type=jnp.float32)
    p = jax.nn.softmax(sc * XA_HEAD_DIM ** -0.5, axis=-1).astype(v.dtype)
    o = jnp.einsum('bhqk,bkhd->bqhd', p, v).reshape(b, s, D_MODEL)
    return o @ wo


def _swiglu(h, w1, w3, w2):
    return (jax.nn.silu(h @ w1) * (h @ w3)) @ w2


def _trunk(x, mem, p):
    for i in range(DEPTH):
        j = i // 2
        h = _rmsnorm(x, p['g_mix'][i])
        if i % 2 == 0:
            x = x + _mla_fnet_mixer(h, p['ab_w_in'][j], p['mla_g_q'][j], p['mla_w_uq'][j],
                                    p['mla_g_kv'][j], p['mla_w_ukv'][j], p['ab_w_out'][j])
        else:
            x = x + _conformer_conv(h, p['conv_w_pw1'][j], p['conv_b_pw1'][j],
                                    p['conv_w_dw'][j], p['conv_b_dw'][j],
                                    p['conv_g_ln'][j], p['conv_b_ln'][j],
                                    p['conv_w_pw2'][j], p['conv_b_pw2'][j])
        x = x + _mem_xattn(_rmsnorm(x, p['g_xq'][i]), _rmsnorm(mem, p['g_xkv'][i]),
                           p['xa_wq'][i], p['xa_wk'][i], p['xa_wv'][i], p['xa_wo'][i])
        x = x + _swiglu(_rmsnorm(x, p['g_ffn'][i]), p['ffn_w1'][i], p['ffn_w3'][i], p['ffn_w2'][i])
    return _rmsnorm(x, p['g_final'])


def setup_inputs(seed: int = 0) -> dict:
    key = jax.random.key(seed)
    ks = iter(jax.random.split(key, 48))

    def w(shape, fan_in):
        return jax.random.normal(next(ks), shape, jnp.float32) * (fan_in ** -0.5)

    def gain(shape):
        return 1.0 + 0.02 * jax.random.normal(next(ks), shape, jnp.float32)

    def bias(shape):
        return 0.02 * jax.random.normal(next(ks), shape, jnp.float32)

    return {
        'x_prompt': jax.random.normal(next(ks), (BATCH, SEQ, D_MODEL), jnp.float32),
        'x_sample': jax.random.normal(next(ks), (DEC_BATCH, DEC_SEQ, D_MODEL), jnp.float32),
        'mem_prompt': jax.random.normal(next(ks), (BATCH, N_MEM, D_MODEL), jnp.float32),
        'mem_sample': jax.random.normal(next(ks), (DEC_BATCH, N_MEM, D_MODEL), jnp.float32),
        'g_mix': gain((DEPTH, D_MODEL)),
        'g_xq': gain((DEPTH, D_MODEL)),
        'g_xkv': gain((DEPTH, D_MODEL)),
        'xa_wq': w((DEPTH, D_MODEL, D_MODEL), D_MODEL),
        'xa_wk': w((DEPTH, D_MODEL, D_MODEL), D_MODEL),
        'xa_wv': w((DEPTH, D_MODEL, D_MODEL), D_MODEL),
        'xa_wo': w((DEPTH, D_MODEL, D_MODEL), D_MODEL),
        'g_ffn': gain((DEPTH, D_MODEL)),
        'ffn_w1': w((DEPTH, D_MODEL, D_FF), D_MODEL),
        'ffn_w3': w((DEPTH, D_MODEL, D_FF), D_MODEL),
        'ffn_w2': w((DEPTH, D_FF, D_MODEL), D_FF),
        'ab_w_in': w((N_EVEN, D_MODEL, IN_W), D_MODEL),
        'mla_g_q': gain((N_EVEN, Q_RANK)),
        'mla_w_uq': w((N_EVEN, Q_RANK, MLA_HEADS * (QK_NOPE + QK_ROPE)), Q_RANK),
        'mla_g_kv': gain((N_EVEN, KV_RANK)),
        'mla_w_ukv': w((N_EVEN, KV_RANK, MLA_HEADS * (QK_NOPE + V_HEAD)), KV_RANK),
        'ab_w_out': w((N_EVEN, MIX_W, D_MODEL), MIX_W),
        'conv_w_pw1': w((N_ODD, D_MODEL, 2 * D_MODEL), D_MODEL),
        'conv_b_pw1': bias((N_ODD, 2 * D_MODEL)),
        'conv_w_dw': w((N_ODD, CONV_K, D_MODEL), CONV_K),
        'conv_b_dw': bias((N_ODD, D_MODEL)),
        'conv_g_ln': gain((N_ODD, D_MODEL)),
        'conv_b_ln': bias((N_ODD, D_MODEL)),
        'conv_w_pw2': w((N_ODD, D_MODEL, D_MODEL), D_MODEL),
        'conv_b_pw2': bias((N_ODD, D_MODEL)),
        'g_final': gain((D_MODEL,)),
    }


def reference(x_prompt, x_sample, mem_prompt, mem_sample,
              g_mix, g_xq, g_xkv, xa_wq, xa_wk, xa_wv, xa_wo,
              g_ffn, ffn_w1, ffn_w3, ffn_w2,
              ab_w_in, mla_g_q, mla_w_uq, mla_g_kv, mla_w_ukv, ab_w_out,
              conv_w_pw1, conv_b_pw1, conv_w_dw, conv_b_dw, conv_g_ln, conv_b_ln,
              conv_w_pw2, conv_b_pw2, g_final):
    p = {
        'g_mix': g_mix, 'g_xq': g_xq, 'g_xkv': g_xkv,
        'xa_wq': xa_wq, 'xa_wk': xa_wk, 'xa_wv': xa_wv, 'xa_wo': xa_wo,
        'g_ffn': g_ffn, 'ffn_w1': ffn_w1, 'ffn_w3': ffn_w3, 'ffn_w2': ffn_w2,
        'ab_w_in': ab_w_in, 'mla_g_q': mla_g_q, 'mla_w_uq': mla_w_uq,
        'mla_g_kv': mla_g_kv, 'mla_w_ukv': mla_w_ukv, 'ab_w_out': ab_w_out,
        'conv_w_pw1': conv_w_pw1, 'conv_b_pw1': conv_b_pw1,
        'conv_w_dw': conv_w_dw, 'conv_b_dw': conv_b_dw,
        'conv_g_ln': conv_g_ln, 'conv_b_ln': conv_b_ln,
        'conv_w_pw2': conv_w_pw2, 'conv_b_pw2': conv_b_pw2,
        'g_final': g_final,
    }
    y_prompt = _trunk(x_prompt, mem_prompt, p)
    y_sample = _trunk(x_sample, mem_sample, p)
    return (y_prompt, y_sample)
```

```python
import contextlib
import os
import numpy as np
import ml_dtypes
import concourse.bass as bass
import concourse.mybir as mybir
from concourse.bass_utils import run_bass_kernel_spmd

F32 = mybir.dt.float32
BF16 = mybir.dt.bfloat16
AF = mybir.ActivationFunctionType
OP = mybir.AluOpType

D = 1024
SP = 16384
SS = 4096
TQ = 512
HALO = 256
NPT = (4096 + 2 * HALO) // TQ
NST = SS // TQ
NT = NPT + 2 * NST
T_OWN = NT * TQ
SEQ_T0 = [0, NPT, NPT + NST]
SEQ_NT = [NPT, NST, NST]
SRC_S = [SP, SS, SS]
EPS = 1e-6
DFF = 2816
NFF = DFF // 128
H = 8
CP = NPT * TQ // 128
NW = 288


def twin(t):
    if t == 0:
        return TQ - NW, NW
    if t == NPT - 1:
        return 0, NW
    return 0, TQ


O_PREPQ = os.environ.get('K_PREPQ', 'sp')
O_KSTRIDE = os.environ.get('K_KSTRIDE', '1') == '1'
O_L0PIPE = os.environ.get('K_L0PIPE', '1') == '1'
O_BGCONV = False


class Sched:
    CE = ('pe', 'act', 'dve', 'pool')

    def __init__(self, nc, es):
        self.nc = nc
        self.eng = dict(pe=nc.tensor, act=nc.scalar, dve=nc.vector, pool=nc.gpsimd, sp=nc.sync)
        self.sem = {e: es.enter_context(nc.semaphore('c_' + e)) for e in self.CE}
        self.cnt = {e: 0 for e in self.CE}
        self.waited = {e: {} for e in self.eng}
        self.free_dsem = [es.enter_context(nc.semaphore('d%d' % i)) for i in range(80)]
        self.dcnt_of_sem = {id(s): 0 for s in self.free_dsem}
        self.dsem = {}
        self.ops = []
        self.n_ops = 0

    def op(self, eng, fn, reads=(), writes=(), dma=None):
        self.ops.append((eng, fn, tuple(reads), tuple(writes), dma))

    def flush(self):
        ops = self.ops
        n = len(ops)
        self.n_ops += n
        signal = [False] * n
        deps = [None] * n
        last_w = {}
        readers = {}
        last_on = {}
        for i, (eng, fn, reads, writes, dk) in enumerate(ops):
            d = set()
            for k in reads:
                j = last_w.get(k)
                if j is not None:
                    d.add(j)
            for k in writes:
                j = last_w.get(k)
                if j is not None:
                    d.add(j)
                r = readers.get(k)
                if r:
                    d.update(r.values())
            dd = []
            for j in d:
                ej, _, rj, wj, dkj = ops[j]
                if dkj is None and dk is None and ej == eng:
                    if eng == 'pe':
                        continue
                    raw = False
                    for k in wj:
                        if k in reads:
                            raw = True
                            break
                    if not raw:
                        continue
                dd.append(j)
                if dkj is None:
                    signal[j] = True
            deps[i] = dd
            for k in reads:
                r = readers.get(k)
                if r is None:
                    r = readers[k] = {}
                r[eng if dk is None else ('d', i)] = i
            for k in writes:
                last_w[k] = i
                readers[k] = {}
            if dk is None:
                last_on[eng] = i
        for e, i in last_on.items():
            signal[i] = True
        val_of = {}
        for i, (eng, fn, reads, writes, dk) in enumerate(ops):
            e = self.eng[eng]
            w = self.waited[eng]
            for j in deps[i]:
                ej, _, _, _, dkj = ops[j]
                if dkj is None:
                    sem = self.sem[ej]
                    val = val_of[j]
                else:
                    sem = self.dsem[dkj]
                    val = self.dcnt_of_sem[id(sem)]
                if w.get(id(sem), 0) < val:
                    e.wait_ge(sem, val)
                    w[id(sem)] = val
            ins = fn(e)
            if dk is not None:
                sem = self.dsem.get(dk)
                if sem is None:
                    sem = self.dsem[dk] = self.free_dsem.pop()
                self.dcnt_of_sem[id(sem)] += 16
                ins.then_inc(sem, 16)
            elif signal[i]:
                self.cnt[eng] += 1
                val_of[i] = self.cnt[eng]
                ins.then_inc(self.sem[eng], 1)
        for eng, e in self.eng.items():
            w = self.waited[eng]
            for ce in self.CE:
                if ce == eng:
                    continue
                v = self.cnt[ce]
                if w.get(id(self.sem[ce]), 0) < v:
                    e.wait_ge(self.sem[ce], v)
                    w[id(self.sem[ce])] = v
            for dk, sem in self.dsem.items():
                v = self.dcnt_of_sem[id(sem)]
                if w.get(id(sem), 0) < v:
                    e.wait_ge(sem, v)
                    w[id(sem)] = v
        for dk, sem in self.dsem.items():
            self.free_dsem.append(sem)
        self.dsem = {}
        self.ops = []


class Phase:
    def __init__(self, K, name):
        self.K = K
        self.S = K.S
        self.nc = K.nc
        self.name = name
        self.es = contextlib.ExitStack()
        self.nalloc = 0

    def __enter__(self):
        self.es.__enter__()
        return self

    def __exit__(self, *a):
        if a[0] is None:
            self.S.flush()
        return self.es.__exit__(*a)

    def sb(self, shape, dt, name=None):
        self.nalloc += 1
        return self.es.enter_context(self.nc.sbuf_tensor('%s_%s%d' % (self.name, name or 't', self.nalloc), list(shape), dt))


class Kern:
    def __init__(self, debug_outs=()):
        self.debug_outs = set(debug_outs)
        self.nc = bass.Bass("TRN2", target_bir_lowering=False)
        self.es = contextlib.ExitStack()
        self.dram = {}

    def din(self, name, shape, dt=F32):
        t = self.nc.dram_tensor(name, list(shape), dt, kind="ExternalInput").ap()
        self.dram[name] = t
        return t

    def dscr(self, name, shape, dt):
        kind = "ExternalOutput" if name in self.debug_outs else "Internal"
        t = self.nc.dram_tensor(name, list(shape), dt, kind=kind).ap()
        self.dram[name] = t
        return t

    def dout(self, name, shape, dt=F32):
        t = self.nc.dram_tensor(name, list(shape), dt, kind="ExternalOutput").ap()
        self.dram[name] = t
        return t

    def mm(self, out, lhsT, rhs, start, stop, reads, writes, tp=None):
        if tp is None:
            self.S.op('pe', lambda e: e.matmul(out, lhsT=lhsT, rhs=rhs, start=start, stop=stop), reads, writes)
        else:
            self.S.op('pe', lambda e: e.matmul(out, lhsT=lhsT, rhs=rhs, start=start, stop=stop, tile_position=tp), reads, writes)

    def tr(self, out, in_, ident, reads, writes):
        self.S.op('pe', lambda e: e.transpose(out, in_, ident), reads, writes)

    def act(self, out, in_, func, reads, writes, bias=None, scale=None):
        kw = {}
        if bias is not None:
            kw['bias'] = bias
        if scale is not None:
            kw['scale'] = scale
        self.S.op('act', lambda e: e.activation(out=out, in_=in_, func=func, **kw), reads, writes)

    def tt(self, eng, out, in0, in1, op, reads, writes):
        self.S.op(eng, lambda e: e.tensor_tensor(out=out, in0=in0, in1=in1, op=op), reads, writes)

    def ts(self, eng, out, in0, s1, op0, reads, writes, s2=None, op1=None):
        if op1 is None:
            self.S.op(eng, lambda e: e.tensor_scalar(out=out, in0=in0, scalar1=s1, scalar2=None, op0=op0), reads, writes)
        else:
            self.S.op(eng, lambda e: e.tensor_scalar(out=out, in0=in0, scalar1=s1, scalar2=s2, op0=op0, op1=op1), reads, writes)

    def stt(self, out, in0, scalar, in1, op0, op1, reads, writes):
        self.S.op('dve', lambda e: e.scalar_tensor_tensor(out=out, in0=in0, scalar=scalar, in1=in1, op0=op0, op1=op1), reads, writes)

    def cp(self, eng, out, in_, reads, writes):
        if eng == 'act':
            self.S.op('act', lambda e: e.activation(out=out, in_=in_, func=AF.Copy), reads, writes)
        else:
            self.S.op(eng, lambda e: e.tensor_copy(out=out, in_=in_), reads, writes)

    def recip(self, out, in_, reads, writes):
        self.S.op('dve', lambda e: e.reciprocal(out=out, in_=in_), reads, writes)

    def memset(self, eng, ap, v, writes):
        self.S.op(eng, lambda e: e.memset(ap, v), (), writes)

    def dma(self, out, in_, reads, writes, key, q='sp'):
        self.S.op(q, lambda e: e.dma_start(out=out, in_=in_), reads, writes, dma=key)

    def build(self):
        nc = self.nc
        with self.es:
            self.es.enter_context(nc.allow_non_contiguous_dma(reason="small strided param loads"))
            self.es.enter_context(nc.allow_low_precision(reason="bf16 matmul operands, fp32 accumulation"))
            self.S = Sched(nc, self.es)
            self.ps = [self.es.enter_context(nc.psum_tensor('ps%d' % i, [128, 512], F32)) for i in range(8)]
            self.declare()
            self.consts()
            self.ph_prep()
            self.ph_mem()
            self.ph_l0(src_only=True)
            self.ph_l0(src_only=False)
            self.ph_attn()
            self.ph_fnet()
            self.ph_mix()
            self.ph_xattn(0, 'x1T', 'x1bT')
            self.ph_ffn(0, 0, 'x1bT', None, 'x1aT')
            self.ph_ffn(0, 1, 'x1bT', 'x1aT', 'x2T')
            self.ph_glu()
            self.ph_conv2()
            self.ph_xattn(1, 'x2bT', 'x3T')
            self.ph_ffn(1, 0, 'x3T', None, 'x3aT')
            self.ph_ffn(1, 1, 'x3T', 'x3aT', None)
        return nc

    def declare(self):
        d = self.din
        d('xo', [T_OWN, D]); d('xp', [SP, D]); d('mem', [3, 256, D])
        d('ropeq', [2, 32, T_OWN]); d('ropek', [2, 32, SP]); d('mask', [128, T_OWN])
        d('identf', [128, 128]); d('w1p', [128, 256], BF16); d('w1s', [32, 64], BF16)
        d('ep', [128, 128, 3 * CP], BF16); d('esm', [128, 32, 384], BF16)
        d('cdft', [128, 256], BF16); d('p4', [128, 32])
        for n, s in [('w_in', [D, 1216]), ('w_uq', [384, 1536]), ('w_ukv', [256, 1024]), ('w_out', [D, D]),
                     ('pw1', [D, 2048]), ('pw2', [D, D])]:
            d(n, s)
        for l in range(2):
            for n in ('wq', 'wk', 'wv', 'wo'):
                d('%s%d' % (n, l), [D, D])
            d('w1_%d' % l, [D, DFF]); d('w3_%d' % l, [D, DFF]); d('w2_%d' % l, [DFF, D])
        for n, s in [('g_mix', [2, D]), ('g_xq', [2, D]), ('g_xkv', [2, D]), ('g_ffn', [2, D]), ('g_q', [384]), ('g_kv', [256]),
                     ('b_pw1', [2048]), ('w_dw', [128, 256]), ('b_dw', [D]), ('g_ln', [D]), ('b_ln', [D]), ('b_pw2', [D]),
                     ('g_final', [D])]:
            d(n, s)
        self.dout('y', [T_OWN, D])
        s = self.dscr
        s('W_in', [128, 8, 1216], BF16); s('W_uq', [128, 3, 1536], BF16); s('W_ukv', [128, 2, 1024], BF16)
        s('W_out', [128, 8, D], BF16); s('W_pw1', [128, 8, 2048], BF16); s('W_pw2', [128, 8, D], BF16)
        for l in range(2):
            for n in ('Wq', 'Wk', 'Wv', 'Wo'):
                s('%s%d' % (n, l), [128, 8, D], BF16)
            s('W1_%d' % l, [128, 8, DFF], BF16); s('W3_%d' % l, [128, 8, DFF], BF16); s('W2_%d' % l, [128, NFF, D], BF16)
        for n in ('x0T', 'x1T', 'x1bT', 'x1aT', 'x2T', 'x2bT', 'x3T', 'x3aT'):
            s(n, [8, 128, T_OWN], F32)
        s('qT', [H, 96, T_OWN], BF16)
        for i, S_ in enumerate(SRC_S):
            s('knT%d' % i, [H, 64, S_], BF16); s('krT%d' % i, [32, S_], BF16)
            s('V1_%d' % i, [H, S_, 128], BF16); s('f%d' % i, [4, S_, 128], BF16)
        s('aT', [4, 128, T_OWN], BF16); s('foT', [4, 128, T_OWN], BF16); s('uT', [8, 128, T_OWN], BF16)
        s('memKT', [2, 3, 128, 8, 256], BF16); s('memV', [2, 3, 128, 2, D], BF16)

    def consts(self):
        nc, es = self.nc, self.es
        self.identf = es.enter_context(nc.sbuf_tensor('identf_sb', [128, 128], F32))
        self.onesb = es.enter_context(nc.sbuf_tensor('onesb', [128, 128], BF16))
        self.identb = es.enter_context(nc.sbuf_tensor('identb', [128, 128], BF16))
        self.dma(self.identf[:], self.dram['identf'][:, :], (), ['identf'], 'identf')
        self.memset('dve', self.onesb[:], 1.0, ['onesb'])
        self.cp('dve', self.identb[:], self.identf[:], ['identf'], ['identb'])
        self.S.flush()

    def prep_jobs(self, early):
        e = [('w_in', ('g_mix', 0), 'W_in'), ('w_uq', ('g_q', None), 'W_uq'), ('w_ukv', ('g_kv', None), 'W_ukv')]
        late = [('w_out', None, 'W_out'), ('pw1', ('g_mix', 1), 'W_pw1'), ('pw2', None, 'W_pw2')]
        for l in range(2):
            e += [('wk%d' % l, ('g_xkv', l), 'Wk%d' % l), ('wv%d' % l, ('g_xkv', l), 'Wv%d' % l)]
            late += [('wq%d' % l, ('g_xq', l), 'Wq%d' % l), ('wo%d' % l, None, 'Wo%d' % l),
                     ('w1_%d' % l, ('g_ffn', l), 'W1_%d' % l), ('w3_%d' % l, ('g_ffn', l), 'W3_%d' % l),
                     ('w2_%d' % l, None, 'W2_%d' % l)]
        return e if early else late

    def make_conv(self, ph, jobs, engs):
        dr = self.dram
        NB = 4
        PF = 3
        st = [ph.sb([128, DFF], F32, 'st') for _ in range(NB)]
        bf = [ph.sb([128, DFF], BF16, 'bf') for _ in range(NB)]
        gt = [ph.sb([128, 8], F32, 'g') for _ in range(len(jobs))]
        items = []
        for ji, (src, g, dst) in enumerate(jobs):
            s_ap = dr[src]
            K_, N_ = s_ap.shape
            KC = K_ // 128
            if g is not None:
                gap = dr[g[0]] if g[1] is None else dr[g[0]][g[1], :]
                self.dma(gt[ji][:, 0:KC], gap.rearrange('(c p) -> p c', p=128), (), [('g', ji)], ('g', ji))
            for kc in range(KC):
                items.append((ji, src, g, dst, kc, N_))

        def ld(it):
            ji, src, g, dst, kc, N_ = items[it]
            b = it % NB
            self.dma(st[b][:, 0:N_], dr[src][kc * 128:(kc + 1) * 128, :], (), [('st', b)], ('st', b))
        for it in range(min(PF, len(items))):
            ld(it)
        state = [0]

        def step():
            it = state[0]
            if it >= len(items):
                return False
            state[0] += 1
            ji, src, g, dst, kc, N_ = items[it]
            b = it % NB
            eng = engs[it % len(engs)]
            if g is None:
                self.cp(eng, bf[b][:, 0:N_], st[b][:, 0:N_], [('st', b)], [('bf', b)])
            elif eng == 'act':
                self.act(bf[b][:, 0:N_], st[b][:, 0:N_], AF.Copy, [('st', b), ('g', ji)], [('bf', b)], scale=gt[ji][:, kc:kc + 1])
            else:
                self.ts(eng, bf[b][:, 0:N_], st[b][:, 0:N_], gt[ji][:, kc:kc + 1], OP.mult, [('st', b), ('g', ji)], [('bf', b)])
            self.dma(dr[dst][:, kc, :], bf[b][:, 0:N_], [('bf', b)], [], ('bf', b), q=O_PREPQ)
            if it + PF < len(items):
                ld(it + PF)
            return True
        return step, len(items)

    def ph_prep(self):
        with Phase(self, 'prep') as ph:
            jobs = self.prep_jobs(True) + ([] if O_BGCONV else self.prep_jobs(False))
            step, n = self.make_conv(ph, jobs, ['dve', 'act'])
            while step():
                pass

    def load_w(self, ph, name, shape, q='pool'):
        t = ph.sb(shape, BF16, name)
        self.dma(t[:], self.dram[name][:], (), [name], name, q=q)
        return t

    def load_vec(self, ph, name, ap, ncol):
        t = ph.sb([128, ncol], F32, name)
        self.dma(t[:], ap.rearrange('(c p) -> p c', p=128), (), [name], name)
        return t

    def rms(self, ph, R, xT, xkey, hT, hkey, nch, n, Dn, bank, src_psum=False, c0=0):
        sq, rs, rstd = R['sq'], R['rs'], R['rstd']
        ps = self.ps[bank]
        self.act(sq[:, 0:nch, 0:n], xT[:, 0:nch, c0:c0 + n], AF.Square, [xkey], ['sq'])
        for c in range(nch):
            self.mm(ps[:, 0:n], self.onesb[:], sq[:, c, 0:n], c == 0, c == nch - 1, ['sq', 'onesb'], [('ps', bank)])
        self.act(rs[:, 0:n], ps[:, 0:n], AF.Ln, [('ps', bank)], ['rs'], bias=R['eps'][:, 0:1], scale=1.0 / Dn)
        self.act(ps[:, 0:n], rs[:, 0:n], AF.Exp, ['rs'], [('ps', bank)], scale=-0.5)
        self.tt('dve', hT[:, 0:nch, c0:c0 + n], xT[:, 0:nch, c0:c0 + n], ps[:, None, 0:n].broadcast_to([128, nch, n]), OP.mult,
                [xkey, ('ps', bank)], [hkey])

    def rms_tiles(self, ph):
        R = dict(sq=ph.sb([128, 8, TQ], BF16, 'sq'), rs=ph.sb([128, TQ], F32, 'rs'), rstd=ph.sb([128, TQ], F32, 'rstd'),
                 eps=ph.sb([128, 1], F32, 'eps'))
        self.memset('dve', R['eps'][:], EPS, ['epsc'])
        return R

    def ph_mem(self):
        dr = self.dram
        with Phase(self, 'mem') as ph:
            R = self.rms_tiles(ph)
            Wk = [self.load_w(ph, 'Wk%d' % l, [128, 8, D]) for l in range(2)]
            Wv = [self.load_w(ph, 'Wv%d' % l, [128, 8, D]) for l in range(2)]
            mtm = [ph.sb([128, 2, D], F32, 'mtm') for _ in range(2)]
            mT = ph.sb([128, 8, 256], F32, 'mT')
            hm = ph.sb([128, 8, 256], BF16, 'hm')
            kT = [ph.sb([128, 8, 256], BF16, 'kT') for _ in range(2)]
            vv = [ph.sb([128, 2, D], BF16, 'vv') for _ in range(2)]
            it = 0
            pb = 0
            for s in range(3):
                b = s % 2
                self.dma(mtm[b][:], dr['mem'][s].rearrange('(s p) d -> p s d', p=128), (), [('mtm', b)], ('mtm', b))
                for c in range(8):
                    bank = pb % 8; pb += 1
                    for sub in range(2):
                        self.tr(self.ps[bank][:, sub * 128:(sub + 1) * 128], mtm[b][:, sub, c * 128:(c + 1) * 128], self.identf[:],
                                [('mtm', b), 'identf'], [('ps', bank)])
                    self.cp('act' if c % 2 else 'dve', mT[:, c, :], self.ps[bank][:, 0:256], [('ps', bank)], ['mT'])
                bank = pb % 8; pb += 1
                self.rms(ph, R, mT, 'mT', hm, 'hm', 8, 256, D, bank)
                for l in range(2):
                    o = it % 2; it += 1
                    for oc in range(8):
                        bank = pb % 8; pb += 1
                        for kc in range(8):
                            self.mm(self.ps[bank][:, 0:256], Wk[l][:, kc, oc * 128:(oc + 1) * 128], hm[:, kc, :], kc == 0, kc == 7,
                                    ['hm', 'Wk%d' % l], [('ps', bank)])
                        self.cp('act' if oc % 2 else 'dve', kT[o][:, oc, :], self.ps[bank][:, 0:256], [('ps', bank)], [('kT', o)])
                    self.dma(dr['memKT'][l, s], kT[o][:], [('kT', o)], [], ('kT', o))
                    for sub in range(2):
                        for half in range(2):
                            bank = pb % 8; pb += 1
                            for kc in range(8):
                                self.mm(self.ps[bank][:], hm[:, kc, sub * 128:(sub + 1) * 128], Wv[l][:, kc, half * 512:(half + 1) * 512],
                                        kc == 0, kc == 7, ['hm', 'Wv%d' % l], [('ps', bank)])
                            self.cp('act' if half else 'dve', vv[o][:, sub, half * 512:(half + 1) * 512], self.ps[bank][:],
                                    [('ps', bank)], [('vv', o)])
                    self.dma(dr['memV'][l, s], vv[o][:], [('vv', o)], [], ('vv', o))

    def ph_l0(self, src_only):
        dr = self.dram
        name = 'l0s' if src_only else 'l0o'
        with Phase(self, name) as ph:
            R = self.rms_tiles(ph)
            W_in = self.load_w(ph, 'W_in', [128, 8, 1216])
            W_uq = self.load_w(ph, 'W_uq', [128, 3, 1536])
            W_ukv = self.load_w(ph, 'W_ukv', [128, 2, 1024])
            xtm = [ph.sb([128, 4, D], F32, 'xtm')]
            xT1 = [ph.sb([128, 8, TQ], F32, 'xT') for _ in range(2)]
            hTs = [ph.sb([128, 8, TQ], BF16, 'hT') for _ in range(2)]
            cq = ph.sb([128, 3, TQ], F32, 'cq')
            cqn = ph.sb([128, 3, TQ], BF16, 'cqn')
            ckv = ph.sb([128, 2, TQ], F32, 'ckv')
            ckvn = ph.sb([128, 2, TQ], BF16, 'ckvn')
            rq = [ph.sb([96, 2, TQ], F32, 'rq') for _ in range(2)]
            rk = [ph.sb([32, 2, TQ], F32, 'rk') for _ in range(2)]
            t1 = ph.sb([96, TQ], F32, 't1'); t2 = ph.sb([96, TQ], F32, 't2')
            qst = [ph.sb([96, H, TQ], BF16, 'qst')]
            krs = [ph.sb([32, TQ], BF16, 'krs') for _ in range(2)]
            kn = [ph.sb([128, 4, TQ], BF16, 'kn') for _ in range(2)]
            vst = [ph.sb([128, 4, H, 128], BF16, 'vst') for _ in range(2)]
            fst = [ph.sb([128, 4, 512], BF16, 'fst') for _ in range(2)]
            for b_ in range(2):
                self.memset('pool', vst[b_][:], 1.0, [('vst', b_)])
            if src_only:
                tiles = [(dr['xp'][i * TQ:(i + 1) * TQ, :], None, 0, i) for i in range(SP // TQ)]
            else:
                tiles = []
                for t in range(NT):
                    sq_ = 0 if t < NPT else (1 if t < NPT + NST else 2)
                    tiles.append((dr['xo'][t * TQ:(t + 1) * TQ, :], t, (sq_ if sq_ > 0 else None), t - SEQ_T0[sq_]))
            pbc = [0]

            def nb():
                b_ = pbc[0] % 8
                pbc[0] += 1
                return b_

            def load_x(i):
                xs, t, src, ti = tiles[i]
                self.dma(xtm[0][:], xs.rearrange('(s p) d -> p s d', p=128), (), [('xtm', 0)], ('xtm', 0))

            def load_r(i):
                xs, t, src, ti = tiles[i]
                b = i % 2
                if t is not None:
                    self.dma(rq[b][64:96, :, :], dr['ropeq'][:, :, t * TQ:(t + 1) * TQ].rearrange('a r t -> r a t'), (), [('rq', b)], ('rq', b))
                if src is not None:
                    self.dma(rk[b][:, :, :], dr['ropek'][:, :, ti * TQ:(ti + 1) * TQ].rearrange('a r t -> r a t'), (), [('rk', b)], ('rk', b))

            def front(i):
                xs, t, src, ti = tiles[i]
                b = i % 2
                X = xT1[b]
                xk = ('xT', b)
                for c in range(8):
                    bank = nb()
                    for sub in range(4):
                        self.tr(self.ps[bank][:, sub * 128:(sub + 1) * 128], xtm[0][:, sub, c * 128:(c + 1) * 128], self.identf[:],
                                [('xtm', 0), 'identf'], [('ps', bank)])
                    self.cp('act' if c % 2 else 'dve', X[:, c, :], self.ps[bank][:], [('ps', bank)], [xk])
                if i + 1 < len(tiles):
                    load_x(i + 1)
                if t is not None:
                    self.dma(dr['x0T'][:, :, t * TQ:(t + 1) * TQ].rearrange('c p t -> p c t'), X[:], [xk], [], xk)
                self.rms(ph, R, X, xk, hTs[b], ('hT', b), 8, TQ, D, nb())

            def back(i):
                xs, t, src, ti = tiles[i]
                b = i % 2
                hT = hTs[b]
                hk = ('hT', b)
                qb = 0
                sb_ = i % 2
                p0 = ti * TQ

                def cq_mm():
                    for oc in range(3):
                        bank = nb()
                        for kc in range(8):
                            self.mm(self.ps[bank][:], W_in[:, kc, oc * 128:(oc + 1) * 128], hT[:, kc, :], kc == 0, kc == 7,
                                    [hk, 'W_in'], [('ps', bank)])
                        self.cp('dve', cq[:, oc, :], self.ps[bank][:], [('ps', bank)], ['cq'])

                def q_heads():
                    for h in range(H):
                        bA = nb(); bB = nb()
                        for kc in range(3):
                            self.mm(self.ps[bA][0:96, :], W_uq[:, kc, h * 192:h * 192 + 96], cqn[:, kc, :], kc == 0, kc == 2,
                                    ['cqn', 'W_uq'], [('ps', bA)])
                        for kc in range(3):
                            self.mm(self.ps[bB][0:96, :], W_uq[:, kc, h * 192 + 96:h * 192 + 192], cqn[:, kc, :], kc == 0, kc == 2,
                                    ['cqn', 'W_uq'], [('ps', bB)])
                        self.cp('act', qst[qb][0:64, h, :], self.ps[bA][0:64, :], [('ps', bA)], [('qst', qb)])
                        self.tt('dve', t1[64:96, :], self.ps[bA][64:96, :], rq[b][64:96, 0, :], OP.mult, [('ps', bA), ('rq', b)], ['t1q'])
                        self.tt('dve', t2[64:96, :], self.ps[bB][64:96, :], rq[b][64:96, 1, :], OP.mult, [('ps', bB), ('rq', b)], ['t2q'])
                        self.tt('pool', qst[qb][64:96, h, :], t1[64:96, :], t2[64:96, :], OP.add, ['t1q', 't2q'], [('qst', qb)])
                    self.dma(dr['qT'][:, :, t * TQ:(t + 1) * TQ].rearrange('h r t -> r h t'), qst[qb][:], [('qst', qb)], [], ('qst', qb))

                def ckv_mm():
                    for oc in range(2):
                        bank = nb()
                        for kc in range(8):
                            self.mm(self.ps[bank][:], W_in[:, kc, 384 + oc * 128:384 + (oc + 1) * 128], hT[:, kc, :], kc == 0, kc == 7,
                                    [hk, 'W_in'], [('ps', bank)])
                        self.cp('dve', ckv[:, oc, :], self.ps[bank][:], [('ps', bank)], ['ckv'])

                def rope_k():
                    bA = nb(); bB = nb()
                    for kc in range(8):
                        self.mm(self.ps[bA][0:32, :], W_in[:, kc, 640:672], hT[:, kc, :], kc == 0, kc == 7, [hk, 'W_in'], [('ps', bA)])
                    for kc in range(8):
                        self.mm(self.ps[bB][0:32, :], W_in[:, kc, 672:704], hT[:, kc, :], kc == 0, kc == 7, [hk, 'W_in'], [('ps', bB)])
                    self.tt('dve', t1[0:32, :], self.ps[bA][0:32, :], rk[b][:, 0, :], OP.mult, [('ps', bA), ('rk', b)], ['t1k'])
                    self.tt('dve', t2[0:32, :], self.ps[bB][0:32, :], rk[b][:, 1, :], OP.mult, [('ps', bB), ('rk', b)], ['t2k'])
                    self.tt('pool', krs[b][:, :], t1[0:32, :], t2[0:32, :], OP.add, ['t1k', 't2k'], [('krs', b)])
                    self.dma(dr['krT%d' % src][:, p0:p0 + TQ], krs[b][:], [('krs', b)], [], ('krs', b))

                def f_mm(subs):
                    for sub in subs:
                        bank = nb()
                        for kc in range(8):
                            self.mm(self.ps[bank][:], hT[:, kc, sub * 128:(sub + 1) * 128], W_in[:, kc, 704:1216], kc == 0, kc == 7,
                                    [hk, 'W_in'], [('ps', bank)])
                        self.cp('dve' if sub % 2 else 'act', fst[sb_][:, sub, :], self.ps[bank][:], [('ps', bank)], [('fst', sb_)])

                def f_store():
                    for sub in range(4):
                        self.dma(dr['f%d' % src][:, p0 + sub * 128:p0 + (sub + 1) * 128, :].rearrange('g p c -> p g c'),
                                 fst[sb_][:, sub, :].rearrange('p (g c) -> p g c', c=128), [('fst', sb_)], [], ('fst', sb_))

                def kn_v():
                    for oc in range(4):
                        bank = nb()
                        for kc in range(2):
                            self.mm(self.ps[bank][:], W_ukv[:, kc, oc * 128:(oc + 1) * 128], ckvn[:, kc, :], kc == 0, kc == 1,
                                    ['ckvn', 'W_ukv'], [('ps', bank)])
                        self.cp('act', kn[sb_][:, oc, :], self.ps[bank][:], [('ps', bank)], [('kn', sb_)])
                    self.dma(dr['knT%d' % src].rearrange('(c g) r s -> (g r) c s', g=2)[:, :, p0:p0 + TQ], kn[sb_][:], [('kn', sb_)], [], ('kn', sb_))
                    for sub in range(4):
                        bank = nb()
                        for kc in range(2):
                            self.mm(self.ps[bank][:], ckvn[:, kc, sub * 128:(sub + 1) * 128], W_ukv[:, kc, 512:1024], kc == 0, kc == 1,
                                    ['ckvn', 'W_ukv'], [('ps', bank)])
                        self.cp('act', vst[sb_][:, sub, :, 0:64], self.ps[bank][:].rearrange('p (h c) -> p h c', c=64), [('ps', bank)], [('vst', sb_)])
                    for sub in range(4):
                        self.dma(dr['V1_%d' % src][:, p0 + sub * 128:p0 + (sub + 1) * 128, :].rearrange('h p c -> p h c'), vst[sb_][:, sub, :, :],
                                 [('vst', sb_)], [], ('vst', sb_))

                doq = t is not None
                dokv = src is not None
                if doq:
                    cq_mm()
                if dokv:
                    ckv_mm()
                if doq:
                    self.rms(ph, R, cq, 'cq', cqn, 'cqn', 3, TQ, 384, nb())
                if dokv:
                    rope_k()
                    f_mm([0, 1])
                    self.rms(ph, R, ckv, 'ckv', ckvn, 'ckvn', 2, TQ, 256, nb())
                    f_mm([2, 3])
                    f_store()
                if doq:
                    q_heads()
                if dokv:
                    kn_v()

            load_x(0)
            load_r(0)
            if len(tiles) > 1:
                load_r(1)
            front(0)
            for i in range(len(tiles)):
                if O_L0PIPE and i + 1 < len(tiles):
                    front(i + 1)
                back(i)
                if (not O_L0PIPE) and i + 1 < len(tiles):
                    front(i + 1)
                if i + 2 < len(tiles):
                    load_r(i + 2)

    def ph_attn(self):
        dr = self.dram
        scale = 96.0 ** -0.5
        with Phase(self, 'attn') as ph:
            ktp = ph.sb([96, SP], BF16, 'ktp'); vp = ph.sb([128, SP // 128, 128], BF16, 'vp')
            kts = [ph.sb([96, SS], BF16, 'kts') for _ in range(2)]
            vs = [ph.sb([128, SS // 128, 128], BF16, 'vs') for _ in range(2)]
            qt = [ph.sb([96, TQ], BF16, 'qt') for _ in range(3)]
            pt = [ph.sb([128, TQ], BF16, 'pt') for _ in range(4)]
            osb = [ph.sb([128, TQ], F32, 'osb') for _ in range(2)]
            rden = ph.sb([64, TQ], F32, 'rden')
            ao = [ph.sb([64, TQ], BF16, 'ao') for _ in range(2)]
            KT = [ktp, kts[0], kts[1]]
            VV = [vp, vs[0], vs[1]]

            def load_kv(h, s):
                S_ = SRC_S[s]
                nseg = S_ // SS
                for g in range(nseg):
                    self.dma(KT[s][0:64, g * SS:(g + 1) * SS], dr['knT%d' % s][h, :, g * SS:(g + 1) * SS], (), [('ktn', s, g)], ('ktn', s, g))
                    self.dma(KT[s][64:96, g * SS:(g + 1) * SS], dr['krT%d' % s][:, g * SS:(g + 1) * SS], (), [('ktr', s, g)], ('ktr', s, g))
                    self.dma(VV[s][:, g * 32:(g + 1) * 32, :], (dr['V1_%d' % s][h, g * SS:(g + 1) * SS, :].rearrange('(p kb) c -> p kb c', kb=32) if O_KSTRIDE else dr['V1_%d' % s][h, g * SS:(g + 1) * SS, :].rearrange('(kb p) c -> p kb c', p=128)),
                             (), [('v', s, g)], ('v', s, g))

            qtl = []
            for h in range(H):
                for s in range(3):
                    for tq in range(SEQ_NT[s]):
                        qtl.append((h, s, tq))
            blocks = []
            for qi_, (h, s, tq) in enumerate(qtl):
                nkb = SRC_S[s] // 128
                for kb in range(nkb):
                    blocks.append((qi_, kb, nkb))
            for s in range(3):
                load_kv(0, s)

            def load_q(qi_):
                h, s, tq = qtl[qi_]
                t = SEQ_T0[s] + tq
                self.dma(qt[qi_ % 3][:], dr['qT'][h, :, t * TQ:(t + 1) * TQ], (), [('qt', qi_ % 3)], ('qt', qi_ % 3))

            NW = 288

            def qwin(qi_):
                h, s, tq = qtl[qi_]
                if s == 0 and tq == 0:
                    return TQ - NW, NW
                if s == 0 and tq == SEQ_NT[0] - 1:
                    return 0, NW
                return 0, TQ
            zt = ph.sb([128, 4, TQ - NW], BF16, 'zt')
            self.memset('dve', zt[:], 0.0, ['zt'])
            t8 = SEQ_NT[0] - 1
            self.dma(dr['aT'][:, :, 0:TQ - NW].rearrange('c p t -> p c t'), zt[:], ['zt'], [], 'zt')
            self.dma(dr['aT'][:, :, t8 * TQ + NW:(t8 + 1) * TQ].rearrange('c p t -> p c t'), zt[:], ['zt'], [], 'zt')

            def epi_tail(qi_):
                h, s, tq = qtl[qi_]
                t = SEQ_T0[s] + tq
                ob = qi_ % 2
                c0, n = qwin(qi_)
                self.mm(self.ps[6][0:64, 0:n], self.identf[:, 64:128], osb[ob][:, 0:n], True, True, [('osb', ob), 'identf'], [('ps', 6)])
                self.recip(rden[:, 0:n], self.ps[6][0:64, 0:n], [('ps', 6)], ['rden'])
                self.tt('dve', ao[ob][:, 0:n], osb[ob][0:64, 0:n], rden[:, 0:n], OP.mult, [('osb', ob), 'rden'], [('ao', ob)])
                self.dma(dr['aT'].rearrange('c (g r) t -> (c g) r t', g=2)[h, :, t * TQ + c0:t * TQ + c0 + n], ao[ob][:, 0:n], [('ao', ob)], [], ('ao', ob))
                if tq == SEQ_NT[s] - 1 and h + 1 < H:
                    load_kv(h + 1, s)

            load_q(0)
            load_q(1)
            pend = []
            tails = []
            nblk = len(blocks)
            for idx in range(nblk + 2):
                if idx < nblk:
                    qi_, kb, nkb = blocks[idx]
                    h, s, tq = qtl[qi_]
                    c0, n = qwin(qi_)
                    if kb == 0 and qi_ + 2 < len(qtl):
                        load_q(qi_ + 2)
                    sb_ = idx % 4
                    pb_ = idx % 4
                    lhs = KT[s][0:96, (kb // 32) * SS + (kb % 32):(kb // 32 + 1) * SS:32] if O_KSTRIDE else KT[s][0:96, kb * 128:(kb + 1) * 128]
                    self.mm(self.ps[sb_][:, 0:n], lhs, qt[qi_ % 3][:, c0:c0 + n], True, True,
                            [('ktn', s, kb // 32), ('ktr', s, kb // 32), ('qt', qi_ % 3)], [('ps', sb_)])
                    self.act(pt[pb_][:, 0:n], self.ps[sb_][:, 0:n], AF.Exp, [('ps', sb_)], [('pt', pb_)], scale=scale)
                    pend.append((qi_, kb, nkb, pb_, s))
                if idx >= 2:
                    q2, k2, n2, p2, s2 = pend.pop(0)
                    obank = 4 + q2 % 2
                    c02, nn2 = qwin(q2)
                    self.mm(self.ps[obank][:, 0:nn2], VV[s2][:, k2, :], pt[p2][:, 0:nn2], k2 == 0, k2 == n2 - 1,
                            [('v', s2, k2 // 32), ('pt', p2)], [('ps', obank)])
                    if k2 == n2 - 1:
                        self.cp('dve', osb[q2 % 2][:, 0:nn2], self.ps[obank][:, 0:nn2], [('ps', obank)], [('osb', q2 % 2)])
                        tails.append((idx + 3, q2))
                while tails and tails[0][0] <= idx:
                    epi_tail(tails.pop(0)[1])
            while tails:
                epi_tail(tails.pop(0)[1])

    def ph_fnet(self):
        dr = self.dram
        with Phase(self, 'fnet') as ph:
            w1p = ph.sb([128, 256], BF16, 'w1p'); w1s = ph.sb([32, 64], BF16, 'w1s')
            cd = ph.sb([128, 256], BF16, 'cd')
            self.dma(w1p[:], dr['w1p'][:, :], (), ['w1p'], 'w1p')
            self.dma(w1s[:], dr['w1s'][:, :], (), ['w1s'], 'w1s')
            self.dma(cd[:], dr['cdft'][:, :], (), ['cd'], 'cd')
            Xs = [ph.sb([128, 128, 128], BF16, 'X') for _ in range(2)]
            Z = ph.sb([128, 128, 256], BF16, 'Z')
            E = ph.sb([128, 128 * 3 * CP], BF16, 'E')
            Y = ph.sb([128, 2, NPT * TQ], BF16, 'Y')
            fo = [ph.sb([128, TQ], BF16, 'fo') for _ in range(2)]
            pb = [0]

            def nb():
                b_ = pb[0] % 8
                pb[0] += 1
                return b_
            foi = 0
            for s in range(3):
                NB_ = SRC_S[s] // 128
                C_ = CP if s == 0 else 128
                ntok = C_ * NB_
                w1 = w1p if s == 0 else w1s
                if s == 0:
                    for q4 in range(4):
                        self.dma(E[:, q4 * 32 * 3 * CP:(q4 + 1) * 32 * 3 * CP],
                                 dr['ep'][:, q4 * 32:(q4 + 1) * 32, :].rearrange('a d c -> a (d c)'), (), [('E', q4)], ('E', q4))
                elif s == 1:
                    for q4 in range(4):
                        self.dma(E[:, q4 * 8 * 384:(q4 + 1) * 8 * 384],
                                 dr['esm'][:, q4 * 8:(q4 + 1) * 8, :].rearrange('a d c -> a (d c)'), (), [('E', q4)], ('E', q4))
                Ev = E[:, 0:NB_ * 3 * C_].rearrange('a (d c) -> a d c', c=3 * C_)
                for g in range(4):
                    it_ = s * 4 + g
                    X = Xs[it_ % 2]
                    xkey = ('X', it_ % 2)
                    if it_ == 0:
                        self.dma(X[0:NB_, :, :], dr['f%d' % s][g].rearrange('(b a) c -> b a c', a=128), (), [xkey], xkey)
                    if it_ + 1 < 12:
                        s2_, g2_ = (it_ + 1) // 4, (it_ + 1) % 4
                        nb2_ = SRC_S[s2_] // 128
                        self.dma(Xs[(it_ + 1) % 2][0:nb2_, :, :], dr['f%d' % s2_][g2_].rearrange('(b a) c -> b a c', a=128), (),
                                 [('X', (it_ + 1) % 2)], ('X', (it_ + 1) % 2))
                    nper = 512 // (2 * NB_)
                    for c0 in range(0, 128, nper):
                        bank = nb()
                        for cc in range(nper):
                            ch = c0 + cc
                            self.mm(self.ps[bank][:, cc * 2 * NB_:(cc + 1) * 2 * NB_], X[0:NB_, :, ch], w1[0:NB_, 0:2 * NB_], True, True,
                                    [xkey, 'w1p', 'w1s'], [('ps', bank)])
                        self.cp('act' if (c0 // nper) % 2 else 'dve', Z[:, c0:c0 + nper, 0:2 * NB_],
                                self.ps[bank][:].rearrange('p (c x) -> p c x', x=2 * NB_), [('ps', bank)], ['Z'])
                    nd = 512 // (2 * C_)
                    for d0 in range(0, NB_, nd):
                        bank = nb()
                        ndd = min(nd, NB_ - d0)
                        for dd in range(ndd):
                            d_ = d0 + dd
                            o = self.ps[bank][:, dd * 2 * C_:(dd + 1) * 2 * C_]
                            self.mm(o, Z[:, :, d_], Ev[:, d_, C_:3 * C_], True, False, ['Z'] + [('E', q_) for q_ in range(4)], [('ps', bank)])
                            self.mm(o, Z[:, :, NB_ + d_], Ev[:, d_, 0:2 * C_], False, True, ['Z'] + [('E', q_) for q_ in range(4)], [('ps', bank)])
                        src_ = self.ps[bank][:, 0:ndd * 2 * C_].rearrange('p (d r c) -> p r c d', r=2, c=C_)
                        dst_ = Y[:, :, 0:ntok].rearrange('p r (c d) -> p r c d', d=NB_)[:, :, :, d0:d0 + ndd]
                        self.cp('act' if (d0 // nd) % 2 else 'dve', dst_, src_, [('ps', bank)], ['Y'])
                    for tq in range(ntok // TQ):
                        bank = nb()
                        self.mm(self.ps[bank][:], cd[:, 0:128], Y[:, 0, tq * TQ:(tq + 1) * TQ], True, False, ['Y', 'cd'], [('ps', bank)])
                        self.mm(self.ps[bank][:], cd[:, 128:256], Y[:, 1, tq * TQ:(tq + 1) * TQ], False, True, ['Y', 'cd'], [('ps', bank)])
                        fb = foi % 2; foi += 1
                        self.cp('act' if tq % 2 else 'dve', fo[fb][:], self.ps[bank][:], [('ps', bank)], [('fo', fb)])
                        t = SEQ_T0[s] + tq
                        self.dma(dr['foT'][g, :, t * TQ:(t + 1) * TQ], fo[fb][:], [('fo', fb)], [], ('fo', fb))

    def xa_rms(self, ph, R, XA, X, xk, b, l, nb):
        self.rms(ph, R, X, xk, XA['hT'][b], ('xhT', b), 8, TQ, D, nb())

    def xa_qproj(self, ph, XA, b, l, nb, ocs):
        hT, qx = XA['hT'][b], XA['qx'][b]
        hk, qk = ('xhT', b), ('qx', b, 0)
        Wq = XA['Wq'][l]
        kq = 'Wq%d' % l
        for oc in ocs:
            bank = nb()
            for kc in range(8):
                self.mm(self.ps[bank][:], Wq[:, kc, oc * 128:(oc + 1) * 128], hT[:, kc, :], kc == 0, kc == 7, [hk, kq], [('ps', bank)])
            self.cp('act' if oc % 2 else 'dve', qx[:, oc, :], self.ps[bank][:], [('ps', bank)], [('qx', b, oc)])

    def xa_back(self, ph, R, XA, X, xk, b, l, s, nb, filler=None):
        qx, ox, ptx, rdx = XA['qx'][b], XA['ox'], XA['pt'], XA['rd']
        Wo, mk, mv = XA['Wo'][l], XA['mk'][l][s], XA['mv'][l][s]
        ko, kmk, kmv = 'Wo%d' % l, 'mk', 'mv'

        def st1(hd):
            for kb in range(2):
                bank = nb()
                for dc in range(2):
                    self.mm(self.ps[bank][:], mk[:, hd * 2 + dc, kb * 128:(kb + 1) * 128], qx[:, hd * 2 + dc, :], dc == 0, dc == 1,
                            [('qx', b, hd * 2 + dc), kmk], [('ps', bank)])
                self.act(ptx[hd % 2][kb][:], self.ps[bank][:], AF.Exp, [('ps', bank)], [('ptx', hd % 2, kb)], scale=1.0 / 16.0)

        def st2(hd):
            p = ptx[hd % 2]
            bank = nb()
            for kb in range(2):
                self.mm(self.ps[bank][:], self.onesb[:], p[kb][:], kb == 0, kb == 1, [('ptx', hd % 2, kb), 'onesb'], [('ps', bank)])
            self.act(rdx[hd % 2][:], self.ps[bank][:], AF.Ln, [('ps', bank)], [('rdx', hd % 2)])
            self.act(rdx[hd % 2][:], rdx[hd % 2][:], AF.Exp, [('rdx', hd % 2)], [('rdx', hd % 2)], scale=-1.0)
            for dv in range(2):
                bank = nb()
                for kb in range(2):
                    self.mm(self.ps[bank][:], mv[:, kb, hd * 256 + dv * 128:hd * 256 + (dv + 1) * 128], p[kb][:], kb == 0, kb == 1,
                            [('ptx', hd % 2, kb), kmv], [('ps', bank)])
                self.tt('dve', ox[:, hd * 2 + dv, :], self.ps[bank][:], rdx[hd % 2][:], OP.mult, [('ps', bank), ('rdx', hd % 2)], [('ox', hd * 2 + dv)])
        st1(0)
        for hd in range(4):
            if hd + 1 < 4:
                st1(hd + 1)
            if filler is not None:
                filler(hd)
            st2(hd)
        for oc in range(8):
            bank = nb()
            for kc in range(8):
                self.mm(self.ps[bank][:], Wo[:, kc, oc * 128:(oc + 1) * 128], ox[:, kc, :], kc == 0, kc == 7, [('ox', kc), ko], [('ps', bank)])
            self.tt('dve', X[:, oc, :], self.ps[bank][:], X[:, oc, :], OP.add, [('ps', bank), xk], [xk])

    def xattn_tiles(self, ph, layers):
        dr = self.dram
        XA = dict(hT=[ph.sb([128, 8, TQ], BF16, 'xhT') for _ in range(2)], qx=[ph.sb([128, 8, TQ], BF16, 'qx') for _ in range(2)],
                  ox=ph.sb([128, 8, TQ], BF16, 'ox'),
                  pt=[[ph.sb([128, TQ], BF16, 'ptx') for _ in range(2)] for _ in range(2)],
                  rd=[ph.sb([128, TQ], F32, 'rdx') for _ in range(2)], Wq={}, Wo={}, mk={}, mv={})
        for l in layers:
            XA['Wq'][l] = self.load_w(ph, 'Wq%d' % l, [128, 8, D])
            XA['Wo'][l] = self.load_w(ph, 'Wo%d' % l, [128, 8, D])
            XA['mk'][l] = []; XA['mv'][l] = []
            for s in range(3):
                mk = ph.sb([128, 8, 256], BF16, 'mk'); mv = ph.sb([128, 2, D], BF16, 'mv')
                self.dma(mk[:], dr['memKT'][l, s], (), ['mk'], ('mk', s))
                self.dma(mv[:], dr['memV'][l, s], (), ['mv'], ('mv', s))
                XA['mk'][l].append(mk); XA['mv'][l].append(mv)
        return XA

    def ph_mix(self):
        dr = self.dram
        with Phase(self, 'mix') as ph:
            W_out = self.load_w(ph, 'W_out', [128, 8, D])
            xT = [ph.sb([128, 8, TQ], F32, 'xT') for _ in range(2)]
            mt = [ph.sb([128, 8, TQ], BF16, 'mt') for _ in range(2)]
            pbc = [0]

            def nb():
                b_ = pbc[0] % 8
                pbc[0] += 1
                return b_

            def load(t):
                b = t % 2
                sl = slice(t * TQ, (t + 1) * TQ)
                self.dma(xT[b][:], dr['x0T'][:, :, sl].rearrange('c p t -> p c t'), (), [('xT', b)], ('xT', b))
                self.dma(mt[b][:, 0:4, :], dr['aT'][:, :, sl].rearrange('c p t -> p c t'), (), [('mta', b)], ('mta', b))
                self.dma(mt[b][:, 4:8, :], dr['foT'][:, :, sl].rearrange('c p t -> p c t'), (), [('mtf', b)], ('mtf', b))
            load(0)
            for t in range(NT):
                b = t % 2
                if t + 1 < NT:
                    load(t + 1)
                X = xT[b]; xk = ('xT', b)
                for oc in range(8):
                    bank = nb()
                    for kc in range(8):
                        self.mm(self.ps[bank][:], W_out[:, kc, oc * 128:(oc + 1) * 128], mt[b][:, kc, :], kc == 0, kc == 7,
                                [('mta', b) if kc < 4 else ('mtf', b), 'W_out'], [('ps', bank)])
                    self.tt('dve', X[:, oc, :], self.ps[bank][:], X[:, oc, :], OP.add, [('ps', bank), xk], [xk])
                self.dma(dr['x1T'][:, :, t * TQ:(t + 1) * TQ].rearrange('c p t -> p c t'), X[:], [xk], [], xk)

    def ph_xattn(self, l, xin, xout):
        dr = self.dram
        with Phase(self, 'xa%d' % l) as ph:
            R = self.rms_tiles(ph)
            XA = self.xattn_tiles(ph, [l])
            xT = [ph.sb([128, 8, TQ], F32, 'xT') for _ in range(3)]
            pbc = [0]

            def nb():
                b_ = pbc[0] % 8
                pbc[0] += 1
                return b_

            def load(t):
                b = t % 3
                self.dma(xT[b][:], dr[xin][:, :, t * TQ:(t + 1) * TQ].rearrange('c p t -> p c t'), (), [('xT', b)], ('xT', b))
            load(0)
            load(1)
            self.xa_rms(ph, R, XA, xT[0], ('xT', 0), 0, l, nb)
            self.xa_qproj(ph, XA, 0, l, nb, range(8))
            for t in range(NT):
                b = t % 3
                if t + 2 < NT:
                    load(t + 2)
                s = 0 if t < NPT else (1 if t < NPT + NST else 2)
                fill = None
                if t + 1 < NT:
                    self.xa_rms(ph, R, XA, xT[(t + 1) % 3], ('xT', (t + 1) % 3), (t + 1) % 2, l, nb)
                    fill = (lambda hd, t=t: self.xa_qproj(ph, XA, (t + 1) % 2, l, nb, [2 * hd, 2 * hd + 1]))
                self.xa_back(ph, R, XA, xT[b], ('xT', b), t % 2, l, s, nb, filler=fill)
                self.dma(dr[xout][:, :, t * TQ:(t + 1) * TQ].rearrange('c p t -> p c t'), xT[b][:], [('xT', b)], [], ('xT', b))

    def ph_ffn(self, l, half, xin, xres, xout):
        dr = self.dram
        NH = NFF // 2
        final = xout is None
        with Phase(self, 'ffn%d%d' % (l, half)) as ph:
            R = self.rms_tiles(ph)
            W1 = ph.sb([128, 8, NH * 128], BF16, 'W1'); W3 = ph.sb([128, 8, NH * 128], BF16, 'W3'); W2 = ph.sb([128, NH, D], BF16, 'W2')
            c0 = half * NH * 128
            self.dma(W1[:], dr['W1_%d' % l][:, :, c0:c0 + NH * 128], (), ['W1'], 'W1', q='pool')
            self.dma(W3[:], dr['W3_%d' % l][:, :, c0:c0 + NH * 128], (), ['W3'], 'W3', q='pool')
            self.dma(W2[:], dr['W2_%d' % l][:, half * NH:(half + 1) * NH, :], (), ['W2'], 'W2', q='pool')
            xT = [ph.sb([128, 8, TQ], F32, 'xT') for _ in range(2)]
            xr = ph.sb([128, 8, TQ], F32, 'xr') if xres else None
            xo = [ph.sb([128, 8, TQ], F32, 'xo') for _ in range(2 if final else 1)]
            for i_, x_ in enumerate(xo):
                self.memset('pool', x_[:], 0.0, [('xo', i_)])
            hTs = [ph.sb([128, 8, TQ], BF16, 'hT') for _ in range(2)]
            aT = ph.sb([128, NH, TQ], BF16, 'aT')
            sg = [ph.sb([128, TQ], BF16, 'sg') for _ in range(2)]
            if final:
                gf = self.load_vec(ph, 'gfin', dr['g_final'], 8)
                ytm = [ph.sb([128, D], F32, 'ytm') for _ in range(2)]
                sq2, rs2, rstd2 = R['sq'], R['rs'], R['rstd']
            pbc = [0]
            yi = [0]

            def nb():
                b_ = pbc[0] % 8
                pbc[0] += 1
                return b_

            def load(t):
                b = t % 2
                self.dma(xT[b][:], dr[xin][:, :, t * TQ:(t + 1) * TQ].rearrange('c p t -> p c t'), (), [('xT', b)], ('xT', b))

            def front(t):
                b = t % 2
                c0, n = twin(t)
                self.rms(ph, R, xT[b], ('xT', b), hTs[b], ('hT', b), 8, n, D, nb(), c0=c0)

            def stA(t, mid=None):
                hT = hTs[t % 2]; hk = ('hT', t % 2)
                c0, n = twin(t)
                for j in range(NH):
                    if j == 2 and mid is not None:
                        mid()
                    b1 = nb(); b3 = nb()
                    for kc in range(8):
                        self.mm(self.ps[b1][:, 0:n], W1[:, kc, j * 128:(j + 1) * 128], hT[:, kc, c0:c0 + n], kc == 0, kc == 7, [hk, 'W1'], [('ps', b1)])
                    for kc in range(8):
                        self.mm(self.ps[b3][:, 0:n], W3[:, kc, j * 128:(j + 1) * 128], hT[:, kc, c0:c0 + n], kc == 0, kc == 7, [hk, 'W3'], [('ps', b3)])
                    sb_ = j % 2
                    self.act(sg[sb_][:, 0:n], self.ps[b1][:, 0:n], AF.Silu, [('ps', b1)], [('sg', sb_)])
                    self.tt('dve', aT[:, j, 0:n], self.ps[b3][:, 0:n], sg[sb_][:, 0:n], OP.mult, [('ps', b3), ('sg', sb_)], ['aT'])

            def stB(t):
                b = t % 2
                XO = xo[t % len(xo)]; xok = ('xo', t % len(xo))
                XR = xr if xres else xT[b]
                xrk = 'xr' if xres else ('xT', b)
                c0, n = twin(t)
                for oc in range(8):
                    bank = nb()
                    for j in range(NH):
                        self.mm(self.ps[bank][:, 0:n], W2[:, j, oc * 128:(oc + 1) * 128], aT[:, j, 0:n], j == 0, j == NH - 1, ['aT', 'W2'], [('ps', bank)])
                    self.tt('dve', XO[:, oc, c0:c0 + n], self.ps[bank][:, 0:n], XR[:, oc, c0:c0 + n], OP.add, [('ps', bank), xrk], [xok])
                if not final:
                    self.dma(dr[xout][:, :, t * TQ:(t + 1) * TQ].rearrange('c p t -> p c t'), XO[:], [xok], [], xok)

            def stF0(t):
                XO = xo[t % 2]; xok = ('xo', t % 2)
                self.act(sq2[:], XO[:], AF.Square, [xok], ['sq'])

            def stF1(t):
                XO = xo[t % 2]; xok = ('xo', t % 2)
                sq, rs, rstd = sq2, rs2, rstd2
                bank = nb()
                for c in range(8):
                    self.mm(self.ps[bank][:], self.onesb[:], sq[:, c, :], c == 0, c == 7, ['sq', 'onesb'], [('ps', bank)])
                self.act(rs[:], self.ps[bank][:], AF.Ln, [('ps', bank)], ['rs'], bias=R['eps'][:, 0:1], scale=1.0 / D)
                self.act(self.ps[bank][:], rs[:], AF.Exp, ['rs'], [('ps', bank)], scale=-0.5)
                for c in range(8):
                    self.stt(XO[:, c, :], XO[:, c, :], gf[:, c:c + 1], self.ps[bank][:], OP.mult, OP.mult, [xok, ('ps', bank), 'gfin'], [xok])

            def stF2(t):
                XO = xo[t % 2]; xok = ('xo', t % 2)
                for sub in range(4):
                    yb = yi[0] % 2; yi[0] += 1
                    for hh in range(2):
                        bank = nb()
                        for c4 in range(4):
                            c = hh * 4 + c4
                            self.tr(self.ps[bank][:, c4 * 128:(c4 + 1) * 128], XO[:, c, sub * 128:(sub + 1) * 128], self.identf[:],
                                    [xok, 'identf'], [('ps', bank)])
                        self.cp('act' if hh else 'dve', ytm[yb][:, hh * 512:(hh + 1) * 512], self.ps[bank][:], [('ps', bank)], [('ytm', yb)])
                    r0 = t * TQ + sub * 128
                    self.dma(dr['y'][r0:r0 + 128, :], ytm[yb][:], [('ytm', yb)], [], ('ytm', yb))

            load(0)
            front(0)
            for t in range(NT):
                if t + 1 < NT:
                    load(t + 1)
                if xres:
                    self.dma(xr[:], dr[xres][:, :, t * TQ:(t + 1) * TQ].rearrange('c p t -> p c t'), (), ['xr'], 'xr')
                stA(t, mid=((lambda t=t: stF1(t - 1)) if (final and t >= 1) else None))
                if t + 1 < NT:
                    front(t + 1)
                if final and t >= 1:
                    stF2(t - 1)
                stB(t)
                if final:
                    stF0(t)
            if final:
                stF1(NT - 1)
                stF2(NT - 1)

    def ph_glu(self):
        dr = self.dram
        with Phase(self, 'glu') as ph:
            R = self.rms_tiles(ph)
            Wp1 = self.load_w(ph, 'W_pw1', [128, 8, 2048])
            bp1 = self.load_vec(ph, 'bp1', dr['b_pw1'], 16)
            xT = [ph.sb([128, 8, TQ], F32, 'xT') for _ in range(3)]
            mk_ = [ph.sb([128, TQ], F32, 'mk') for _ in range(3)]
            U = [ph.sb([128, 8, TQ], BF16, 'U') for _ in range(2)]
            hTs = [ph.sb([128, 8, TQ], BF16, 'hT') for _ in range(2)]
            sgt = [ph.sb([128, TQ], F32, 'sgt') for _ in range(2)]
            utmp = [ph.sb([128, TQ], F32, 'utmp') for _ in range(2)]
            pbc = [0]

            def nb():
                b_ = pbc[0] % 8
                pbc[0] += 1
                return b_

            def load(t):
                sl = slice(t * TQ, (t + 1) * TQ)
                self.dma(xT[t % 3][:], dr['x2T'][:, :, sl].rearrange('c p t -> p c t'), (), [('xT', t % 3)], ('xT', t % 3))
                if t in (0, NPT - 1):
                    self.dma(mk_[t % 3][:], dr['mask'][:, sl], (), [('mk', t % 3)], ('mk', t % 3))

            def front(t):
                self.rms(ph, R, xT[t % 3], ('xT', t % 3), hTs[t % 2], ('hT', t % 2), 8, TQ, D, nb())
            load(0)
            load(1)
            front(0)
            for t in range(NT):
                b = t % 2
                uk = ('U', b)
                hT = hTs[b]; hk = ('hT', b)
                if t + 2 < NT:
                    load(t + 2)
                if t + 1 < NT:
                    front(t + 1)
                for oc in range(8):
                    ba = nb(); bg = nb()
                    for kc in range(8):
                        self.mm(self.ps[ba][:], Wp1[:, kc, oc * 128:(oc + 1) * 128], hT[:, kc, :], kc == 0, kc == 7, [hk, 'W_pw1'], [('ps', ba)])
                    for kc in range(8):
                        self.mm(self.ps[bg][:], Wp1[:, kc, 1024 + oc * 128:1024 + (oc + 1) * 128], hT[:, kc, :], kc == 0, kc == 7,
                                [hk, 'W_pw1'], [('ps', bg)])
                    i2 = oc % 2
                    self.act(sgt[i2][:], self.ps[bg][:], AF.Sigmoid, [('ps', bg), 'bp1'], [('sgt', i2)], bias=bp1[:, 8 + oc:9 + oc])
                    if t in (0, NPT - 1):
                        self.stt(utmp[i2][:], self.ps[ba][:], bp1[:, oc:oc + 1], sgt[i2][:], OP.add, OP.mult, [('ps', ba), ('sgt', i2), 'bp1'], [('utmp', i2)])
                        self.tt('pool', U[b][:, oc, :], utmp[i2][:], mk_[t % 3][:], OP.mult, [('utmp', i2), ('mk', t % 3)], [uk])
                    else:
                        self.stt(U[b][:, oc, :], self.ps[ba][:], bp1[:, oc:oc + 1], sgt[i2][:], OP.add, OP.mult, [('ps', ba), ('sgt', i2), 'bp1'], [uk])
                self.dma(dr['uT'][:, :, t * TQ:(t + 1) * TQ].rearrange('c p t -> p c t'), U[b][:], [uk], [], uk)

    def ph_conv2(self):
        dr = self.dram
        with Phase(self, 'conv') as ph:
            eps_t = ph.sb([128, 1], F32, 'eps')
            self.memset('dve', eps_t[:], EPS, ['epsc'])
            Wp2 = self.load_w(ph, 'W_pw2', [128, 8, D])
            bdw = self.load_vec(ph, 'bdw', dr['b_dw'], 8)
            gln = self.load_vec(ph, 'gln', dr['g_ln'], 8)
            bln = self.load_vec(ph, 'bln', dr['b_ln'], 8)
            bp2 = self.load_vec(ph, 'bp2', dr['b_pw2'], 8)
            wsel = ph.sb([128, 8, 4, 8], F32, 'wsel')
            self.dma(wsel[:].rearrange('p c b g -> p (c b g)'), dr['w_dw'][:, :], (), ['wsel'], 'wsel')
            p4 = ph.sb([128, 32], F32, 'p4')
            self.dma(p4[:], dr['p4'][:, :], (), ['p4'], 'p4')
            W4 = ph.sb([128, 8, 4, 8, 32], BF16, 'W4')
            ii = 0
            for c in range(8):
                for cb in range(4):
                    self.tt('dve' if ii % 2 else 'pool', W4[:, c, cb, :, :], p4[:, None, :].broadcast_to([128, 8, 32]),
                            wsel[:, c, cb, :, None].broadcast_to([128, 8, 32]), OP.mult, ['p4', 'wsel'], ['W4'])
                    ii += 1
            XT = ph.sb([128, 8, TQ], F32, 'xT'); xk = 'xT'
            UW = TQ + 28
            ur = [ph.sb([128, 8, 4, UW], BF16, 'ur') for _ in range(2)]
            cvs = [ph.sb([128, 8, TQ], F32, 'cv') for _ in range(2)]
            cvbs = [ph.sb([128, 8, TQ], BF16, 'cvb') for _ in range(2)]
            sqbs = [ph.sb([128, 8, TQ], BF16, 'sqb') for _ in range(2)]
            mean = ph.sb([128, TQ], F32, 'mean'); msq = ph.sb([128, TQ], F32, 'msq'); var = ph.sb([128, TQ], F32, 'var')
            sd = ph.sb([128, TQ], F32, 'sd'); rstd2 = ph.sb([128, TQ], F32, 'rstd2')
            pbc = [0]

            def nb():
                b_ = pbc[0] % 8
                pbc[0] += 1
                return b_

            def flags(t):
                s = 0 if t < NPT else (1 if t < NPT + NST else 2)
                return (t == SEQ_T0[s]), (t == SEQ_T0[s] + SEQ_NT[s] - 1)

            def load_u(t):
                b = t % 2
                first, last = flags(t)
                if first:
                    self.memset('pool', ur[b][:, :, :, 0:16], 0.0, [('ur', b, r_) for r_ in range(4)])
                if last:
                    self.memset('pool', ur[b][:, :, :, 524:UW], 0.0, [('ur', b, r_) for r_ in range(4)])
                for r in range(4):
                    lo = (15 - r) if first else 0
                    hi = (TQ + 15 - r) if last else UW
                    g0 = t * TQ - 15 + r
                    self.dma(ur[b][32 * r:32 * r + 32, :, :, lo:hi],
                             dr['uT'][:, :, g0 + lo:g0 + hi].rearrange('c (b p) t -> p c b t', p=32), (), [('ur', b, r)], ('ur', b, r),
                             q=('sp' if r % 2 == 0 else 'act'))

            def stA(t):
                b = t % 2
                U = ur[b]; uks = [('ur', b, r_) for r_ in range(4)]
                cv, cvb, sqb = cvs[b], cvbs[b], sqbs[b]
                for oc in range(8):
                    bank = nb()
                    for G in range(8):
                        for cb in range(4):
                            self.mm(self.ps[bank][32 * cb:32 * cb + 32, :], W4[:, oc, cb, G, :], U[:, oc, cb, 4 * G:4 * G + TQ], G == 0, G == 7,
                                    uks + ['W4'], [('ps', bank)], tp=((0, 96) if cb == 3 else None))
                    self.act(cv[:, oc, :], self.ps[bank][:], AF.Identity, [('ps', bank), 'bdw'], [('cv', b)], bias=bdw[:, oc:oc + 1])
                    self.act(sqb[:, oc, :], self.ps[bank][:], AF.Square, [('ps', bank), 'bdw'], [('sqb', b)], bias=bdw[:, oc:oc + 1])
                    self.cp('pool' if oc % 2 else 'dve', cvb[:, oc, :], cv[:, oc, :], [('cv', b)], [('cvb', b), ('z', b)])

            def stB(t):
                b = t % 2
                cv, cvb, sqb = cvs[b], cvbs[b], sqbs[b]
                ck, cbk, sk = ('cv', b), ('cvb', b), ('sqb', b)
                self.dma(XT[:], dr['x2T'][:, :, t * TQ:(t + 1) * TQ].rearrange('c p t -> p c t'), (), [xk], xk)
                b1 = nb(); b2 = nb()
                for c in range(8):
                    self.mm(self.ps[b1][:], self.onesb[:], cvb[:, c, :], c == 0, c == 7, [cbk, 'onesb'], [('ps', b1)])
                for c in range(8):
                    self.mm(self.ps[b2][:], self.onesb[:], sqb[:, c, :], c == 0, c == 7, [sk, 'onesb'], [('ps', b2)])
                self.act(mean[:], self.ps[b1][:], AF.Copy, [('ps', b1)], ['mean'], scale=1.0 / D)
                self.tt('dve', msq[:], mean[:], mean[:], OP.mult, ['mean'], ['msq'])
                self.stt(var[:], self.ps[b2][:], 1.0 / D, msq[:], OP.mult, OP.subtract, [('ps', b2), 'msq'], ['var'])
                self.act(sd[:], var[:], AF.Sqrt, ['var'], ['sd'], bias=eps_t[:, 0:1])
                self.recip(rstd2[:], sd[:], ['sd'], ['rstd2'])
                for oc in range(8):
                    e_ = 'pool' if oc % 2 else 'dve'
                    ckc = ('cvc', b, oc)
                    self.tt(e_, cv[:, oc, :], cv[:, oc, :], mean[:], OP.subtract, [ck, 'mean'], [ckc])
                    self.tt(e_, cv[:, oc, :], cv[:, oc, :], rstd2[:], OP.mult, [ckc, 'rstd2'], [ckc])
                    self.act(cvb[:, oc, :], cv[:, oc, :], AF.Silu, [ckc, 'gln', 'bln'], [('z', b), cbk], bias=bln[:, oc:oc + 1], scale=gln[:, oc:oc + 1])
                for oc in range(8):
                    bank = nb()
                    for kc in range(8):
                        self.mm(self.ps[bank][:], Wp2[:, kc, oc * 128:(oc + 1) * 128], cvb[:, kc, :], kc == 0, kc == 7, [('z', b), 'W_pw2'], [('ps', bank)])
                    self.stt(XT[:, oc, :], self.ps[bank][:], bp2[:, oc:oc + 1], XT[:, oc, :], OP.add, OP.add, [('ps', bank), xk, 'bp2'], [xk])
                self.dma(dr['x2bT'][:, :, t * TQ:(t + 1) * TQ].rearrange('c p t -> p c t'), XT[:], [xk], [], xk)

            load_u(0)
            load_u(1)
            stA(0)
            for t in range(NT):
                if t + 1 < NT:
                    stA(t + 1)
                if t + 2 < NT:
                    load_u(t + 2)
                stB(t)


def _bf(a):
    return np.asarray(a, dtype=np.float32).astype(ml_dtypes.bfloat16)


def _const_tables():
    c = {}
    c['identf'] = np.eye(128, dtype=np.float32)
    c['p4'] = np.ascontiguousarray(np.tile(np.eye(32, dtype=np.float32), (4, 1)))
    for nm, nbk, S_ in (('w1p', 128, SP), ('w1s', 32, SS)):
        b = np.arange(nbk)[:, None].astype(np.float64); d = np.arange(nbk)[None, :].astype(np.float64)
        th = 2 * np.pi * ((b * d) % nbk) / nbk
        c[nm] = _bf(np.concatenate([np.cos(th), -np.sin(th)], axis=1) / np.sqrt(S_))
    ch = np.arange(128)[:, None].astype(np.float64); m = np.arange(128)[None, :].astype(np.float64)
    ph_ = 2 * np.pi * ((ch * m) % 128) / 128
    c['cdft'] = _bf(np.concatenate([np.cos(ph_), np.sin(ph_)], axis=1) / np.sqrt(128.0))
    a = np.arange(128, dtype=np.int64)[:, None, None]; d = np.arange(32, dtype=np.int64)[None, :, None]; cc = np.arange(128, dtype=np.int64)[None, None, :]
    k = 32 * cc + d
    th = 2 * np.pi * ((k * a) % SS).astype(np.float64) / SS
    c['esm'] = _bf(np.concatenate([np.sin(th), np.cos(th), -np.sin(th)], axis=2))
    return c


def _ep_table(j):
    a = np.arange(128, dtype=np.int64)[:, None, None]; d = np.arange(128, dtype=np.int64)[None, :, None]
    cc = ((32 * j - 2 + np.arange(CP, dtype=np.int64)) % 128)[None, None, :]
    k = 128 * cc + d
    th = 2 * np.pi * ((k * a) % SP).astype(np.float64) / SP
    return _bf(np.concatenate([np.sin(th), np.cos(th), -np.sin(th)], axis=2))


def _rope_tab(pos):
    half = 16
    inv = (10000.0 ** (-np.arange(half, dtype=np.float32) / half)).astype(np.float32)
    ang = pos.astype(np.float32)[None, :] * inv[:, None]
    cos = np.cos(ang).astype(np.float32); sin = np.sin(ang).astype(np.float32)
    cos2 = np.concatenate([cos, cos], axis=0)
    sinp = np.concatenate([-sin, sin], axis=0)
    return np.stack([cos2, sinp], axis=0)


_NC_CACHE = {}


def _prep_inputs(inp):
    f = lambda k: np.asarray(inp[k], dtype=np.float32)
    shared = {}
    w_in = f('ab_w_in')[0]
    kr = w_in[:, 640:672]
    shared['w_in'] = np.ascontiguousarray(np.concatenate([w_in[:, 0:672], kr[:, 16:32], kr[:, 0:16], w_in[:, 672:1184]], axis=1))
    wuq = f('mla_w_uq')[0].reshape(384, H, 96)
    A = wuq
    B = np.concatenate([wuq[:, :, 0:64], wuq[:, :, 80:96], wuq[:, :, 64:80]], axis=2)
    shared['w_uq'] = np.ascontiguousarray(np.concatenate([A, B], axis=2).reshape(384, H * 192))
    wukv = f('mla_w_ukv')[0].reshape(256, H, 128)
    shared['w_ukv'] = np.ascontiguousarray(np.concatenate([wukv[:, :, 0:64].reshape(256, 512), wukv[:, :, 64:128].reshape(256, 512)], axis=1))
    shared['w_out'] = f('ab_w_out')[0]
    shared['pw1'] = f('conv_w_pw1')[0]; shared['pw2'] = f('conv_w_pw2')[0]
    for l in range(2):
        shared['wq%d' % l] = f('xa_wq')[l]; shared['wk%d' % l] = f('xa_wk')[l]
        shared['wv%d' % l] = f('xa_wv')[l]; shared['wo%d' % l] = f('xa_wo')[l]
        shared['w1_%d' % l] = f('ffn_w1')[l]; shared['w3_%d' % l] = f('ffn_w3')[l]; shared['w2_%d' % l] = f('ffn_w2')[l]
    shared['g_mix'] = f('g_mix'); shared['g_xq'] = f('g_xq'); shared['g_xkv'] = f('g_xkv'); shared['g_ffn'] = f('g_ffn')
    shared['g_q'] = f('mla_g_q')[0]; shared['g_kv'] = f('mla_g_kv')[0]
    shared['b_pw1'] = f('conv_b_pw1')[0]; w32 = np.concatenate([f('conv_w_dw')[0], np.zeros((1, D), np.float32)], axis=0)
    shared['w_dw'] = np.ascontiguousarray(w32.reshape(8, 4, 8, 4, 32).transpose(1, 4, 2, 3, 0).reshape(128, 256))
    shared['b_dw'] = f('conv_b_dw')[0]
    shared['g_ln'] = f('conv_g_ln')[0]; shared['b_ln'] = f('conv_b_ln')[0]; shared['b_pw2'] = f('conv_b_pw2')[0]
    shared['g_final'] = f('g_final')
    shared.update(_const_tables())
    shared['ropek'] = _rope_tab(np.arange(SP))
    xp_all = f('x_prompt'); xs_all = f('x_sample'); mp = f('mem_prompt'); ms = f('mem_sample')
    maps = []
    for c in range(8):
        seq, j = c // 4, c % 4
        pos = 4096 * j - HALO + np.arange(NPT * TQ)
        idx = pos % SP
        m = dict(shared)
        m['xo'] = np.ascontiguousarray(np.concatenate([xp_all[seq][idx], xs_all[2 * c], xs_all[2 * c + 1]], axis=0))
        m['xp'] = xp_all[seq]
        m['mem'] = np.ascontiguousarray(np.stack([mp[seq], ms[2 * c], ms[2 * c + 1]], axis=0))
        posq = np.concatenate([idx, np.arange(SS), np.arange(SS)])
        m['ropeq'] = _rope_tab(posq)
        valid = ((pos >= 0) & (pos < SP)).astype(np.float32)
        mk = np.concatenate([valid, np.ones(2 * SS, np.float32)])
        m['mask'] = np.ascontiguousarray(np.broadcast_to(mk[None, :], (128, T_OWN)))
        m['ep'] = _ep_table(j)
        maps.append(m)
    return maps


def _assemble(results):
    yp = np.zeros((2, SP, D), np.float32)
    ys = np.zeros((16, SS, D), np.float32)
    for c in range(8):
        y = results[c]['y']
        seq, j = c // 4, c % 4
        yp[seq, 4096 * j:4096 * (j + 1)] = y[HALO:HALO + 4096]
        ys[2 * c] = y[NPT * TQ:NPT * TQ + SS]
        ys[2 * c + 1] = y[NPT * TQ + SS:NPT * TQ + 2 * SS]
    return yp, ys


def kernel(**inputs):
    if 'nc' not in _NC_CACHE:
        _NC_CACHE['nc'] = Kern().build()
    nc = _NC_CACHE['nc']
    maps = _prep_inputs(inputs)
    res = run_bass_kernel_spmd(nc, maps, core_ids=list(range(8)))
    return _assemble(res.results)
```

```python
import contextlib
import os
import numpy as np
import ml_dtypes
import concourse.bass as bass
import concourse.mybir as mybir
from concourse.bass_utils import run_bass_kernel_spmd

F32 = mybir.dt.float32
BF16 = mybir.dt.bfloat16
AF = mybir.ActivationFunctionType
OP = mybir.AluOpType

D = 1024
SP = 16384
SS = 4096
TQ = 512
HALO = 256
NPT = (4096 + 2 * HALO) // TQ
NST = SS // TQ
NT = NPT + 2 * NST
T_OWN = NT * TQ
SEQ_T0 = [0, NPT, NPT + NST]
SEQ_NT = [NPT, NST, NST]
SRC_S = [SP, SS, SS]
EPS = 1e-6
DFF = 2816
NFF = DFF // 128
H = 8
CP = NPT * TQ // 128
NW = 288


def twin(t):
    if t == 0:
        return TQ - NW, NW
    if t == NPT - 1:
        return 0, NW
    return 0, TQ


O_PREPQ = os.environ.get('K_PREPQ', 'sp')
O_KSTRIDE = os.environ.get('K_KSTRIDE', '1') == '1'
O_L0PIPE = os.environ.get('K_L0PIPE', '1') == '1'
O_BGCONV = False


class Sched:
    CE = ('pe', 'act', 'dve', 'pool')

    def __init__(self, nc, es):
        self.nc = nc
        self.eng = dict(pe=nc.tensor, act=nc.scalar, dve=nc.vector, pool=nc.gpsimd, sp=nc.sync)
        self.sem = {e: es.enter_context(nc.semaphore('c_' + e)) for e in self.CE}
        self.cnt = {e: 0 for e in self.CE}
        self.waited = {e: {} for e in self.eng}
        self.free_dsem = [es.enter_context(nc.semaphore('d%d' % i)) for i in range(80)]
        self.dcnt_of_sem = {id(s): 0 for s in self.free_dsem}
        self.dsem = {}
        self.ops = []
        self.n_ops = 0

    def op(self, eng, fn, reads=(), writes=(), dma=None):
        self.ops.append((eng, fn, tuple(reads), tuple(writes), dma))

    def flush(self):
        ops = self.ops
        n = len(ops)
        self.n_ops += n
        signal = [False] * n
        deps = [None] * n
        last_w = {}
        readers = {}
        last_on = {}
        for i, (eng, fn, reads, writes, dk) in enumerate(ops):
            d = set()
            for k in reads:
                j = last_w.get(k)
                if j is not None:
                    d.add(j)
            for k in writes:
                j = last_w.get(k)
                if j is not None:
                    d.add(j)
                r = readers.get(k)
                if r:
                    d.update(r.values())
            dd = []
            for j in d:
                ej, _, rj, wj, dkj = ops[j]
                if dkj is None and dk is None and ej == eng:
                    if eng == 'pe':
                        continue
                    raw = False
                    for k in wj:
                        if k in reads:
                            raw = True
                            break
                    if not raw:
                        continue
                dd.append(j)
                if dkj is None:
                    signal[j] = True
            deps[i] = dd
            for k in reads:
                r = readers.get(k)
                if r is None:
                    r = readers[k] = {}
                r[eng if dk is None else ('d', i)] = i
            for k in writes:
                last_w[k] = i
                readers[k] = {}
            if dk is None:
                last_on[eng] = i
        for e, i in last_on.items():
            signal[i] = True
        val_of = {}
        for i, (eng, fn, reads, writes, dk) in enumerate(ops):
            e = self.eng[eng]
            w = self.waited[eng]
            for j in deps[i]:
                ej, _, _, _, dkj = ops[j]
                if dkj is None:
                    sem = self.sem[ej]
                    val = val_of[j]
                else:
                    sem = self.dsem[dkj]
                    val = self.dcnt_of_sem[id(sem)]
                if w.get(id(sem), 0) < val:
                    e.wait_ge(sem, val)
                    w[id(sem)] = val
            ins = fn(e)
            if dk is not None:
                sem = self.dsem.get(dk)
                if sem is None:
                    sem = self.dsem[dk] = self.free_dsem.pop()
                self.dcnt_of_sem[id(sem)] += 16
                ins.then_inc(sem, 16)
            elif signal[i]:
                self.cnt[eng] += 1
                val_of[i] = self.cnt[eng]
                ins.then_inc(self.sem[eng], 1)
        for eng, e in self.eng.items():
            w = self.waited[eng]
            for ce in self.CE:
                if ce == eng:
                    continue
                v = self.cnt[ce]
                if w.get(id(self.sem[ce]), 0) < v:
                    e.wait_ge(self.sem[ce], v)
                    w[id(self.sem[ce])] = v
            for dk, sem in self.dsem.items():
                v = self.dcnt_of_sem[id(sem)]
                if w.get(id(sem), 0) < v:
                    e.wait_ge(sem, v)
                    w[id(sem)] = v
        for dk, sem in self.dsem.items():
            self.free_dsem.append(sem)
        self.dsem = {}
        self.ops = []


class Phase:
    def __init__(self, K, name):
        self.K = K
        self.S = K.S
        self.nc = K.nc
        self.name = name
        self.es = contextlib.ExitStack()
        self.nalloc = 0

    def __enter__(self):
        self.es.__enter__()
        return self

    def __exit__(self, *a):
        if a[0] is None:
            self.S.flush()
        return self.es.__exit__(*a)

    def sb(self, shape, dt, name=None):
        self.nalloc += 1
        return self.es.enter_context(self.nc.sbuf_tensor('%s_%s%d' % (self.name, name or 't', self.nalloc), list(shape), dt))


class Kern:
    def __init__(self, debug_outs=()):
        self.debug_outs = set(debug_outs)
        self.nc = bass.Bass("TRN2", target_bir_lowering=False)
        self.es = contextlib.ExitStack()
        self.dram = {}

    def din(self, name, shape, dt=F32):
        t = self.nc.dram_tensor(name, list(shape), dt, kind="ExternalInput").ap()
        self.dram[name] = t
        return t

    def dscr(self, name, shape, dt):
        kind = "ExternalOutput" if name in self.debug_outs else "Internal"
        t = self.nc.dram_tensor(name, list(shape), dt, kind=kind).ap()
        self.dram[name] = t
        return t

    def dout(self, name, shape, dt=F32):
        t = self.nc.dram_tensor(name, list(shape), dt, kind="ExternalOutput").ap()
        self.dram[name] = t
        return t

    def mm(self, out, lhsT, rhs, start, stop, reads, writes, tp=None):
        if tp is None:
            self.S.op('pe', lambda e: e.matmul(out, lhsT=lhsT, rhs=rhs, start=start, stop=stop), reads, writes)
        else:
            self.S.op('pe', lambda e: e.matmul(out, lhsT=lhsT, rhs=rhs, start=start, stop=stop, tile_position=tp), reads, writes)

    def tr(self, out, in_, ident, reads, writes):
        self.S.op('pe', lambda e: e.transpose(out, in_, ident), reads, writes)

    def act(self, out, in_, func, reads, writes, bias=None, scale=None):
        kw = {}
        if bias is not None:
            kw['bias'] = bias
        if scale is not None:
            kw['scale'] = scale
        self.S.op('act', lambda e: e.activation(out=out, in_=in_, func=func, **kw), reads, writes)

    def tt(self, eng, out, in0, in1, op, reads, writes):
        self.S.op(eng, lambda e: e.tensor_tensor(out=out, in0=in0, in1=in1, op=op), reads, writes)

    def ts(self, eng, out, in0, s1, op0, reads, writes, s2=None, op1=None):
        if op1 is None:
            self.S.op(eng, lambda e: e.tensor_scalar(out=out, in0=in0, scalar1=s1, scalar2=None, op0=op0), reads, writes)
        else:
            self.S.op(eng, lambda e: e.tensor_scalar(out=out, in0=in0, scalar1=s1, scalar2=s2, op0=op0, op1=op1), reads, writes)

    def stt(self, out, in0, scalar, in1, op0, op1, reads, writes):
        self.S.op('dve', lambda e: e.scalar_tensor_tensor(out=out, in0=in0, scalar=scalar, in1=in1, op0=op0, op1=op1), reads, writes)

    def cp(self, eng, out, in_, reads, writes):
        if eng == 'act':
            self.S.op('act', lambda e: e.activation(out=out, in_=in_, func=AF.Copy), reads, writes)
        else:
            self.S.op(eng, lambda e: e.tensor_copy(out=out, in_=in_), reads, writes)

    def recip(self, out, in_, reads, writes):
        self.S.op('dve', lambda e: e.reciprocal(out=out, in_=in_), reads, writes)

    def memset(self, eng, ap, v, writes):
        self.S.op(eng, lambda e: e.memset(ap, v), (), writes)

    def dma(self, out, in_, reads, writes, key, q='sp'):
        self.S.op(q, lambda e: e.dma_start(out=out, in_=in_), reads, writes, dma=key)

    def build(self):
        nc = self.nc
        with self.es:
            self.es.enter_context(nc.allow_non_contiguous_dma(reason="small strided param loads"))
            self.es.enter_context(nc.allow_low_precision(reason="bf16 matmul operands, fp32 accumulation"))
            self.S = Sched(nc, self.es)
            self.ps = [self.es.enter_context(nc.psum_tensor('ps%d' % i, [128, 512], F32)) for i in range(8)]
            self.declare()
            self.consts()
            self.ph_prep()
            self.ph_mem()
            self.ph_l0(src_only=True)
            self.ph_l0(src_only=False)
            self.ph_attn()
            self.ph_fnet()
            self.ph_mix()
            self.ph_xattn(0, 'x1T', 'x1bT')
            self.ph_ffn(0, 0, 'x1bT', None, 'x1aT')
            self.ph_ffn(0, 1, 'x1bT', 'x1aT', 'x2T')
            self.ph_glu()
            self.ph_conv2()
            self.ph_xattn(1, 'x2bT', 'x3T')
            self.ph_ffn(1, 0, 'x3T', None, 'x3aT')
            self.ph_ffn(1, 1, 'x3T', 'x3aT', None)
        return nc

    def declare(self):
        d = self.din
        d('xo', [T_OWN, D]); d('xp', [SP, D]); d('mem', [3, 256, D])
        d('ropeq', [2, 32, T_OWN]); d('ropek', [2, 32, SP]); d('mask', [128, T_OWN])
        d('identf', [128, 128]); d('w1p', [128, 256], BF16); d('w1s', [32, 64], BF16)
        d('ep', [128, 128, 3 * CP], BF16); d('esm', [128, 32, 384], BF16)
        d('cdft', [128, 256], BF16); d('p4', [128, 32])
        for n, s in [('w_in', [D, 1216]), ('w_uq', [384, 1536]), ('w_ukv', [256, 1024]), ('w_out', [D, D]),
                     ('pw1', [D, 2048]), ('pw2', [D, D])]:
            d(n, s)
        for l in range(2):
            for n in ('wq', 'wk', 'wv', 'wo'):
                d('%s%d' % (n, l), [D, D])
            d('w1_%d' % l, [D, DFF]); d('w3_%d' % l, [D, DFF]); d('w2_%d' % l, [DFF, D])
        for n, s in [('g_mix', [2, D]), ('g_xq', [2, D]), ('g_xkv', [2, D]), ('g_ffn', [2, D]), ('g_q', [384]), ('g_kv', [256]),
                     ('b_pw1', [2048]), ('w_dw', [128, 256]), ('b_dw', [D]), ('g_ln', [D]), ('b_ln', [D]), ('b_pw2', [D]),
                     ('g_final', [D])]:
            d(n, s)
        self.dout('y', [T_OWN, D])
        s = self.dscr
        s('W_in', [128, 8, 1216], BF16); s('W_uq', [128, 3, 1536], BF16); s('W_ukv', [128, 2, 1024], BF16)
        s('W_out', [128, 8, D], BF16); s('W_pw1', [128, 8, 2048], BF16); s('W_pw2', [128, 8, D], BF16)
        for l in range(2):
            for n in ('Wq', 'Wk', 'Wv', 'Wo'):
                s('%s%d' % (n, l), [128, 8, D], BF16)
            s('W1_%d' % l, [128, 8, DFF], BF16); s('W3_%d' % l, [128, 8, DFF], BF16); s('W2_%d' % l, [128, NFF, D], BF16)
        for n in ('x0T', 'x1T', 'x1bT', 'x1aT', 'x2T', 'x2bT', 'x3T', 'x3aT'):
            s(n, [8, 128, T_OWN], F32)
        s('qT', [H, 96, T_OWN], BF16)
        for i, S_ in enumerate(SRC_S):
            s('knT%d' % i, [H, 64, S_], BF16); s('krT%d' % i, [32, S_], BF16)
            s('V1_%d' % i, [H, S_, 128], BF16); s('f%d' % i, [4, S_, 128], BF16)
        s('aT', [4, 128, T_OWN], BF16); s('foT', [4, 128, T_OWN], BF16); s('uT', [8, 128, T_OWN], BF16)
        s('memKT', [2, 3, 128, 8, 256], BF16); s('memV', [2, 3, 128, 2, D], BF16)

    def consts(self):
        nc, es = self.nc, self.es
        self.identf = es.enter_context(nc.sbuf_tensor('identf_sb', [128, 128], F32))
        self.onesb = es.enter_context(nc.sbuf_tensor('onesb', [128, 128], BF16))
        self.identb = es.enter_context(nc.sbuf_tensor('identb', [128, 128], BF16))
        self.dma(self.identf[:], self.dram['identf'][:, :], (), ['identf'], 'identf')
        self.memset('dve', self.onesb[:], 1.0, ['onesb'])
        self.cp('dve', self.identb[:], self.identf[:], ['identf'], ['identb'])
        self.S.flush()

    def prep_jobs(self, early):
        e = [('w_in', ('g_mix', 0), 'W_in'), ('w_uq', ('g_q', None), 'W_uq'), ('w_ukv', ('g_kv', None), 'W_ukv')]
        late = [('w_out', None, 'W_out'), ('pw1', ('g_mix', 1), 'W_pw1'), ('pw2', None, 'W_pw2')]
        for l in range(2):
            e += [('wk%d' % l, ('g_xkv', l), 'Wk%d' % l), ('wv%d' % l, ('g_xkv', l), 'Wv%d' % l)]
            late += [('wq%d' % l, ('g_xq', l), 'Wq%d' % l), ('wo%d' % l, None, 'Wo%d' % l),
                     ('w1_%d' % l, ('g_ffn', l), 'W1_%d' % l), ('w3_%d' % l, ('g_ffn', l), 'W3_%d' % l),
                     ('w2_%d' % l, None, 'W2_%d' % l)]
        return e if early else late

    def make_conv(self, ph, jobs, engs):
        dr = self.dram
        NB = 4
        PF = 3
        st = [ph.sb([128, DFF], F32, 'st') for _ in range(NB)]
        bf = [ph.sb([128, DFF], BF16, 'bf') for _ in range(NB)]
        gt = [ph.sb([128, 8], F32, 'g') for _ in range(len(jobs))]
        items = []
        for ji, (src, g, dst) in enumerate(jobs):
            s_ap = dr[src]
            K_, N_ = s_ap.shape
            KC = K_ // 128
            if g is not None:
                gap = dr[g[0]] if g[1] is None else dr[g[0]][g[1], :]
                self.dma(gt[ji][:, 0:KC], gap.rearrange('(c p) -> p c', p=128), (), [('g', ji)], ('g', ji))
            for kc in range(KC):
                items.append((ji, src, g, dst, kc, N_))

        def ld(it):
            ji, src, g, dst, kc, N_ = items[it]
            b = it % NB
            self.dma(st[b][:, 0:N_], dr[src][kc * 128:(kc + 1) * 128, :], (), [('st', b)], ('st', b))
        for it in range(min(PF, len(items))):
            ld(it)
        state = [0]

        def step():
            it = state[0]
            if it >= len(items):
                return False
            state[0] += 1
            ji, src, g, dst, kc, N_ = items[it]
            b = it % NB
            eng = engs[it % len(engs)]
            if g is None:
                self.cp(eng, bf[b][:, 0:N_], st[b][:, 0:N_], [('st', b)], [('bf', b)])
            elif eng == 'act':
                self.act(bf[b][:, 0:N_], st[b][:, 0:N_], AF.Copy, [('st', b), ('g', ji)], [('bf', b)], scale=gt[ji][:, kc:kc + 1])
            else:
                self.ts(eng, bf[b][:, 0:N_], st[b][:, 0:N_], gt[ji][:, kc:kc + 1], OP.mult, [('st', b), ('g', ji)], [('bf', b)])
            self.dma(dr[dst][:, kc, :], bf[b][:, 0:N_], [('bf', b)], [], ('bf', b), q=O_PREPQ)
            if it + PF < len(items):
                ld(it + PF)
            return True
        return step, len(items)

    def ph_prep(self):
        with Phase(self, 'prep') as ph:
            jobs = self.prep_jobs(True) + ([] if O_BGCONV else self.prep_jobs(False))
            step, n = self.make_conv(ph, jobs, ['dve', 'act'])
            while step():
                pass

    def load_w(self, ph, name, shape, q='pool'):
        t = ph.sb(shape, BF16, name)
        self.dma(t[:], self.dram[name][:], (), [name], name, q=q)
        return t

    def load_vec(self, ph, name, ap, ncol):
        t = ph.sb([128, ncol], F32, name)
        self.dma(t[:], ap.rearrange('(c p) -> p c', p=128), (), [name], name)
        return t

    def rms(self, ph, R, xT, xkey, hT, hkey, nch, n, Dn, bank, src_psum=False, c0=0):
        sq, rs, rstd = R['sq'], R['rs'], R['rstd']
        ps = self.ps[bank]
        self.act(sq[:, 0:nch, 0:n], xT[:, 0:nch, c0:c0 + n], AF.Square, [xkey], ['sq'])
        for c in range(nch):
            self.mm(ps[:, 0:n], self.onesb[:], sq[:, c, 0:n], c == 0, c == nch - 1, ['sq', 'onesb'], [('ps', bank)])
        self.act(rs[:, 0:n], ps[:, 0:n], AF.Ln, [('ps', bank)], ['rs'], bias=R['eps'][:, 0:1], scale=1.0 / Dn)
        self.act(ps[:, 0:n], rs[:, 0:n], AF.Exp, ['rs'], [('ps', bank)], scale=-0.5)
        self.tt('dve', hT[:, 0:nch, c0:c0 + n], xT[:, 0:nch, c0:c0 + n], ps[:, None, 0:n].broadcast_to([128, nch, n]), OP.mult,
                [xkey, ('ps', bank)], [hkey])

    def rms_tiles(self, ph):
        R = dict(sq=ph.sb([128, 8, TQ], BF16, 'sq'), rs=ph.sb([128, TQ], F32, 'rs'), rstd=ph.sb([128, TQ], F32, 'rstd'),
                 eps=ph.sb([128, 1], F32, 'eps'))
        self.memset('dve', R['eps'][:], EPS, ['epsc'])
        return R

    def ph_mem(self):
        dr = self.dram
        with Phase(self, 'mem') as ph:
            R = self.rms_tiles(ph)
            Wk = [self.load_w(ph, 'Wk%d' % l, [128, 8, D]) for l in range(2)]
            Wv = [self.load_w(ph, 'Wv%d' % l, [128, 8, D]) for l in range(2)]
            mtm = [ph.sb([128, 2, D], F32, 'mtm') for _ in range(2)]
            mT = ph.sb([128, 8, 256], F32, 'mT')
            hm = ph.sb([128, 8, 256], BF16, 'hm')
            kT = [ph.sb([128, 8, 256], BF16, 'kT') for _ in range(2)]
            vv = [ph.sb([128, 2, D], BF16, 'vv') for _ in range(2)]
            it = 0
            pb = 0
            for s in range(3):
                b = s % 2
                self.dma(mtm[b][:], dr['mem'][s].rearrange('(s p) d -> p s d', p=128), (), [('mtm', b)], ('mtm', b))
                for c in range(8):
                    bank = pb % 8; pb += 1
                    for sub in range(2):
                        self.tr(self.ps[bank][:, sub * 128:(sub + 1) * 128], mtm[b][:, sub, c * 128:(c + 1) * 128], self.identf[:],
                                [('mtm', b), 'identf'], [('ps', bank)])
                    self.cp('act' if c % 2 else 'dve', mT[:, c, :], self.ps[bank][:, 0:256], [('ps', bank)], ['mT'])
                bank = pb % 8; pb += 1
                self.rms(ph, R, mT, 'mT', hm, 'hm', 8, 256, D, bank)
                for l in range(2):
                    o = it % 2; it += 1
                    for oc in range(8):
                        bank = pb % 8; pb += 1
                        for kc in range(8):
                            self.mm(self.ps[bank][:, 0:256], Wk[l][:, kc, oc * 128:(oc + 1) * 128], hm[:, kc, :], kc == 0, kc == 7,
                                    ['hm', 'Wk%d' % l], [('ps', bank)])
                        self.cp('act' if oc % 2 else 'dve', kT[o][:, oc, :], self.ps[bank][:, 0:256], [('ps', bank)], [('kT', o)])
                    self.dma(dr['memKT'][l, s], kT[o][:], [('kT', o)], [], ('kT', o))
                    for sub in range(2):
                        for half in range(2):
                            bank = pb % 8; pb += 1
                            for kc in range(8):
                                self.mm(self.ps[bank][:], hm[:, kc, sub * 128:(sub + 1) * 128], Wv[l][:, kc, half * 512:(half + 1) * 512],
                                        kc == 0, kc == 7, ['hm', 'Wv%d' % l], [('ps', bank)])
                            self.cp('act' if half else 'dve', vv[o][:, sub, half * 512:(half + 1) * 512], self.ps[bank][:],
                                    [('ps', bank)], [('vv', o)])
                    self.dma(dr['memV'][l, s], vv[o][:], [('vv', o)], [], ('vv', o))

    def ph_l0(self, src_only):
        dr = self.dram
        name = 'l0s' if src_only else 'l0o'
        with Phase(self, name) as ph:
            R = self.rms_tiles(ph)
            W_in = self.load_w(ph, 'W_in', [128, 8, 1216])
            W_uq = self.load_w(ph, 'W_uq', [128, 3, 1536])
            W_ukv = self.load_w(ph, 'W_ukv', [128, 2, 1024])
            xtm = [ph.sb([128, 4, D], F32, 'xtm')]
            xT1 = [ph.sb([128, 8, TQ], F32, 'xT') for _ in range(2)]
            hTs = [ph.sb([128, 8, TQ], BF16, 'hT') for _ in range(2)]
            cq = ph.sb([128, 3, TQ], F32, 'cq')
            cqn = ph.sb([128, 3, TQ], BF16, 'cqn')
            ckv = ph.sb([128, 2, TQ], F32, 'ckv')
            ckvn = ph.sb([128, 2, TQ], BF16, 'ckvn')
            rq = [ph.sb([96, 2, TQ], F32, 'rq') for _ in range(2)]
            rk = [ph.sb([32, 2, TQ], F32, 'rk') for _ in range(2)]
            t1 = ph.sb([96, TQ], F32, 't1'); t2 = ph.sb([96, TQ], F32, 't2')
            qst = [ph.sb([96, H, TQ], BF16, 'qst')]
            krs = [ph.sb([32, TQ], BF16, 'krs') for _ in range(2)]
            kn = [ph.sb([128, 4, TQ], BF16, 'kn') for _ in range(2)]
            vst = [ph.sb([128, 4, H, 128], BF16, 'vst') for _ in range(2)]
            fst = [ph.sb([128, 4, 512], BF16, 'fst') for _ in range(2)]
            for b_ in range(2):
                self.memset('pool', vst[b_][:], 1.0, [('vst', b_)])
            if src_only:
                tiles = [(dr['xp'][i * TQ:(i + 1) * TQ, :], None, 0, i) for i in range(SP // TQ)]
            else:
                tiles = []
                for t in range(NT):
                    sq_ = 0 if t < NPT else (1 if t < NPT + NST else 2)
                    tiles.append((dr['xo'][t * TQ:(t + 1) * TQ, :], t, (sq_ if sq_ > 0 else None), t - SEQ_T0[sq_]))
            pbc = [0]

            def nb():
                b_ = pbc[0] % 8
                pbc[0] += 1
                return b_

            def load_x(i):
                xs, t, src, ti = tiles[i]
                self.dma(xtm[0][:], xs.rearrange('(s p) d -> p s d', p=128), (), [('xtm', 0)], ('xtm', 0))

            def load_r(i):
                xs, t, src, ti = tiles[i]
                b = i % 2
                if t is not None:
                    self.dma(rq[b][64:96, :, :], dr['ropeq'][:, :, t * TQ:(t + 1) * TQ].rearrange('a r t -> r a t'), (), [('rq', b)], ('rq', b))
                if src is not None:
                    self.dma(rk[b][:, :, :], dr['ropek'][:, :, ti * TQ:(ti + 1) * TQ].rearrange('a r t -> r a t'), (), [('rk', b)], ('rk', b))

            def front(i):
                xs, t, src, ti = tiles[i]
                b = i % 2
                X = xT1[b]
                xk = ('xT', b)
                for c in range(8):
                    bank = nb()
                    for sub in range(4):
                        self.tr(self.ps[bank][:, sub * 128:(sub + 1) * 128], xtm[0][:, sub, c * 128:(c + 1) * 128], self.identf[:],
                                [('xtm', 0), 'identf'], [('ps', bank)])
                    self.cp('act' if c % 2 else 'dve', X[:, c, :], self.ps[bank][:], [('ps', bank)], [xk])
                if i + 1 < len(tiles):
                    load_x(i + 1)
                if t is not None:
                    self.dma(dr['x0T'][:, :, t * TQ:(t + 1) * TQ].rearrange('c p t -> p c t'), X[:], [xk], [], xk)
                self.rms(ph, R, X, xk, hTs[b], ('hT', b), 8, TQ, D, nb())

            def back(i):
                xs, t, src, ti = tiles[i]
                b = i % 2
                hT = hTs[b]
                hk = ('hT', b)
                qb = 0
                sb_ = i % 2
                p0 = ti * TQ

                def cq_mm():
                    for oc in range(3):
                        bank = nb()
                        for kc in range(8):
                            self.mm(self.ps[bank][:], W_in[:, kc, oc * 128:(oc + 1) * 128], hT[:, kc, :], kc == 0, kc == 7,
                                    [hk, 'W_in'], [('ps', bank)])
                        self.cp('dve', cq[:, oc, :], self.ps[bank][:], [('ps', bank)], ['cq'])

                def q_heads():
                    for h in range(H):
                        bA = nb(); bB = nb()
                        for kc in range(3):
                            self.mm(self.ps[bA][0:96, :], W_uq[:, kc, h * 192:h * 192 + 96], cqn[:, kc, :], kc == 0, kc == 2,
                                    ['cqn', 'W_uq'], [('ps', bA)])
                        for kc in range(3):
                            self.mm(self.ps[bB][0:96, :], W_uq[:, kc, h * 192 + 96:h * 192 + 192], cqn[:, kc, :], kc == 0, kc == 2,
                                    ['cqn', 'W_uq'], [('ps', bB)])
                        self.cp('act', qst[qb][0:64, h, :], self.ps[bA][0:64, :], [('ps', bA)], [('qst', qb)])
                        self.tt('dve', t1[64:96, :], self.ps[bA][64:96, :], rq[b][64:96, 0, :], OP.mult, [('ps', bA), ('rq', b)], ['t1q'])
                        self.tt('dve', t2[64:96, :], self.ps[bB][64:96, :], rq[b][64:96, 1, :], OP.mult, [('ps', bB), ('rq', b)], ['t2q'])
                        self.tt('pool', qst[qb][64:96, h, :], t1[64:96, :], t2[64:96, :], OP.add, ['t1q', 't2q'], [('qst', qb)])
                    self.dma(dr['qT'][:, :, t * TQ:(t + 1) * TQ].rearrange('h r t -> r h t'), qst[qb][:], [('qst', qb)], [], ('qst', qb))

                def ckv_mm():
                    for oc in range(2):
                        bank = nb()
                        for kc in range(8):
                            self.mm(self.ps[bank][:], W_in[:, kc, 384 + oc * 128:384 + (oc + 1) * 128], hT[:, kc, :], kc == 0, kc == 7,
                                    [hk, 'W_in'], [('ps', bank)])
                        self.cp('dve', ckv[:, oc, :], self.ps[bank][:], [('ps', bank)], ['ckv'])

                def rope_k():
                    bA = nb(); bB = nb()
                    for kc in range(8):
                        self.mm(self.ps[bA][0:32, :], W_in[:, kc, 640:672], hT[:, kc, :], kc == 0, kc == 7, [hk, 'W_in'], [('ps', bA)])
                    for kc in range(8):
                        self.mm(self.ps[bB][0:32, :], W_in[:, kc, 672:704], hT[:, kc, :], kc == 0, kc == 7, [hk, 'W_in'], [('ps', bB)])
                    self.tt('dve', t1[0:32, :], self.ps[bA][0:32, :], rk[b][:, 0, :], OP.mult, [('ps', bA), ('rk', b)], ['t1k'])
                    self.tt('dve', t2[0:32, :], self.ps[bB][0:32, :], rk[b][:, 1, :], OP.mult, [('ps', bB), ('rk', b)], ['t2k'])
                    self.tt('pool', krs[b][:, :], t1[0:32, :], t2[0:32, :], OP.add, ['t1k', 't2k'], [('krs', b)])
                    self.dma(dr['krT%d' % src][:, p0:p0 + TQ], krs[b][:], [('krs', b)], [], ('krs', b))

                def f_mm(subs):
                    for sub in subs:
                        bank = nb()
                        for kc in range(8):
                            self.mm(self.ps[bank][:], hT[:, kc, sub * 128:(sub + 1) * 128], W_in[:, kc, 704:1216], kc == 0, kc == 7,
                                    [hk, 'W_in'], [('ps', bank)])
                        self.cp('dve' if sub % 2 else 'act', fst[sb_][:, sub, :], self.ps[bank][:], [('ps', bank)], [('fst', sb_)])

                def f_store():
                    for sub in range(4):
                        self.dma(dr['f%d' % src][:, p0 + sub * 128:p0 + (sub + 1) * 128, :].rearrange('g p c -> p g c'),
                                 fst[sb_][:, sub, :].rearrange('p (g c) -> p g c', c=128), [('fst', sb_)], [], ('fst', sb_))

                def kn_v():
                    for oc in range(4):
                        bank = nb()
                        for kc in range(2):
                            self.mm(self.ps[bank][:], W_ukv[:, kc, oc * 128:(oc + 1) * 128], ckvn[:, kc, :], kc == 0, kc == 1,
                                    ['ckvn', 'W_ukv'], [('ps', bank)])
                        self.cp('act', kn[sb_][:, oc, :], self.ps[bank][:], [('ps', bank)], [('kn', sb_)])
                    self.dma(dr['knT%d' % src].rearrange('(c g) r s -> (g r) c s', g=2)[:, :, p0:p0 + TQ], kn[sb_][:], [('kn', sb_)], [], ('kn', sb_))
                    for sub in range(4):
                        bank = nb()
                        for kc in range(2):
                            self.mm(self.ps[bank][:], ckvn[:, kc, sub * 128:(sub + 1) * 128], W_ukv[:, kc, 512:1024], kc == 0, kc == 1,
                                    ['ckvn', 'W_ukv'], [('ps', bank)])
                        self.cp('act', vst[sb_][:, sub, :, 0:64], self.ps[bank][:].rearrange('p (h c) -> p h c', c=64), [('ps', bank)], [('vst', sb_)])
                    for sub in range(4):
                        self.dma(dr['V1_%d' % src][:, p0 + sub * 128:p0 + (sub + 1) * 128, :].rearrange('h p c -> p h c'), vst[sb_][:, sub, :, :],
                                 [('vst', sb_)], [], ('vst', sb_))

                doq = t is not None
                dokv = src is not None
                if doq:
                    cq_mm()
                if dokv:
                    ckv_mm()
                if doq:
                    self.rms(ph, R, cq, 'cq', cqn, 'cqn', 3, TQ, 384, nb())
                if dokv:
                    rope_k()
                    f_mm([0, 1])
                    self.rms(ph, R, ckv, 'ckv', ckvn, 'ckvn', 2, TQ, 256, nb())
                    f_mm([2, 3])
                    f_store()
                if doq:
                    q_heads()
                if dokv:
                    kn_v()

            load_x(0)
            load_r(0)
            if len(tiles) > 1:
                load_r(1)
            front(0)
            for i in range(len(tiles)):
                if O_L0PIPE and i + 1 < len(tiles):
                    front(i + 1)
                back(i)
                if (not O_L0PIPE) and i + 1 < len(tiles):
                    front(i + 1)
                if i + 2 < len(tiles):
                    load_r(i + 2)

    def ph_attn(self):
        dr = self.dram
        scale = 96.0 ** -0.5
        with Phase(self, 'attn') as ph:
            ktp = ph.sb([96, SP], BF16, 'ktp'); vp = ph.sb([128, SP // 128, 128], BF16, 'vp')
            kts = [ph.sb([96, SS], BF16, 'kts') for _ in range(2)]
            vs = [ph.sb([128, SS // 128, 128], BF16, 'vs') for _ in range(2)]
            qt = [ph.sb([96, TQ], BF16, 'qt') for _ in range(3)]
            pt = [ph.sb([128, TQ], BF16, 'pt') for _ in range(4)]
            osb = [ph.sb([128, TQ], F32, 'osb') for _ in range(2)]
            rden = ph.sb([64, TQ], F32, 'rden')
            ao = [ph.sb([64, TQ], BF16, 'ao') for _ in range(2)]
            KT = [ktp, kts[0], kts[1]]
            VV = [vp, vs[0], vs[1]]

            def load_kv(h, s):
                S_ = SRC_S[s]
                nseg = S_ // SS
                for g in range(nseg):
                    self.dma(KT[s][0:64, g * SS:(g + 1) * SS], dr['knT%d' % s][h, :, g * SS:(g + 1) * SS], (), [('ktn', s, g)], ('ktn', s, g))
                    self.dma(KT[s][64:96, g * SS:(g + 1) * SS], dr['krT%d' % s][:, g * SS:(g + 1) * SS], (), [('ktr', s, g)], ('ktr', s, g))
                    self.dma(VV[s][:, g * 32:(g + 1) * 32, :], (dr['V1_%d' % s][h, g * SS:(g + 1) * SS, :].rearrange('(p kb) c -> p kb c', kb=32) if O_KSTRIDE else dr['V1_%d' % s][h, g * SS:(g + 1) * SS, :].rearrange('(kb p) c -> p kb c', p=128)),
                             (), [('v', s, g)], ('v', s, g))

            qtl = []
            for h in range(H):
                for s in range(3):
                    for tq in range(SEQ_NT[s]):
                        qtl.append((h, s, tq))
            blocks = []
            for qi_, (h, s, tq) in enumerate(qtl):
                nkb = SRC_S[s] // 128
                for kb in range(nkb):
                    blocks.append((qi_, kb, nkb))
            for s in range(3):
                load_kv(0, s)

            def load_q(qi_):
                h, s, tq = qtl[qi_]
                t = SEQ_T0[s] + tq
                self.dma(qt[qi_ % 3][:], dr['qT'][h, :, t * TQ:(t + 1) * TQ], (), [('qt', qi_ % 3)], ('qt', qi_ % 3))

            NW = 288

            def qwin(qi_):
                h, s, tq = qtl[qi_]
                if s == 0 and tq == 0:
                    return TQ - NW, NW
                if s == 0 and tq == SEQ_NT[0] - 1:
                    return 0, NW
                return 0, TQ
            zt = ph.sb([128, 4, TQ - NW], BF16, 'zt')
            self.memset('dve', zt[:], 0.0, ['zt'])
            t8 = SEQ_NT[0] - 1
            self.dma(dr['aT'][:, :, 0:TQ - NW].rearrange('c p t -> p c t'), zt[:], ['zt'], [], 'zt')
            self.dma(dr['aT'][:, :, t8 * TQ + NW:(t8 + 1) * TQ].rearrange('c p t -> p c t'), zt[:], ['zt'], [], 'zt')

            def epi_tail(qi_):
                h, s, tq = qtl[qi_]
                t = SEQ_T0[s] + tq
                ob = qi_ % 2
                c0, n = qwin(qi_)
                self.mm(self.ps[6][0:64, 0:n], self.identf[:, 64:128], osb[ob][:, 0:n], True, True, [('osb', ob), 'identf'], [('ps', 6)])
                self.recip(rden[:, 0:n], self.ps[6][0:64, 0:n], [('ps', 6)], ['rden'])
                self.tt('dve', ao[ob][:, 0:n], osb[ob][0:64, 0:n], rden[:, 0:n], OP.mult, [('osb', ob), 'rden'], [('ao', ob)])
                self.dma(dr['aT'].rearrange('c (g r) t -> (c g) r t', g=2)[h, :, t * TQ + c0:t * TQ + c0 + n], ao[ob][:, 0:n], [('ao', ob)], [], ('ao', ob))
                if tq == SEQ_NT[s] - 1 and h + 1 < H:
                    load_kv(h + 1, s)

            load_q(0)
            load_q(1)
            pend = []
            tails = []
            nblk = len(blocks)
            for idx in range(nblk + 2):
                if idx < nblk:
                    qi_, kb, nkb = blocks[idx]
                    h, s, tq = qtl[qi_]
                    c0, n = qwin(qi_)
                    if kb == 0 and qi_ + 2 < len(qtl):
                        load_q(qi_ + 2)
                    sb_ = idx % 4
                    pb_ = idx % 4
                    lhs = KT[s][0:96, (kb // 32) * SS + (kb % 32):(kb // 32 + 1) * SS:32] if O_KSTRIDE else KT[s][0:96, kb * 128:(kb + 1) * 128]
                    self.mm(self.ps[sb_][:, 0:n], lhs, qt[qi_ % 3][:, c0:c0 + n], True, True,
                            [('ktn', s, kb // 32), ('ktr', s, kb // 32), ('qt', qi_ % 3)], [('ps', sb_)])
                    self.act(pt[pb_][:, 0:n], self.ps[sb_][:, 0:n], AF.Exp, [('ps', sb_)], [('pt', pb_)], scale=scale)
                    pend.append((qi_, kb, nkb, pb_, s))
                if idx >= 2:
                    q2, k2, n2, p2, s2 = pend.pop(0)
                    obank = 4 + q2 % 2
                    c02, nn2 = qwin(q2)
                    self.mm(self.ps[obank][:, 0:nn2], VV[s2][:, k2, :], pt[p2][:, 0:nn2], k2 == 0, k2 == n2 - 1,
                            [('v', s2, k2 // 32), ('pt', p2)], [('ps', obank)])
                    if k2 == n2 - 1:
                        self.cp('dve', osb[q2 % 2][:, 0:nn2], self.ps[obank][:, 0:nn2], [('ps', obank)], [('osb', q2 % 2)])
                        tails.append((idx + 3, q2))
                while tails and tails[0][0] <= idx:
                    epi_tail(tails.pop(0)[1])
            while tails:
                epi_tail(tails.pop(0)[1])

    def ph_fnet(self):
        dr = self.dram
        with Phase(self, 'fnet') as ph:
            w1p = ph.sb([128, 256], BF16, 'w1p'); w1s = ph.sb([32, 64], BF16, 'w1s')
            cd = ph.sb([128, 256], BF16, 'cd')
            self.dma(w1p[:], dr['w1p'][:, :], (), ['w1p'], 'w1p')
            self.dma(w1s[:], dr['w1s'][:, :], (), ['w1s'], 'w1s')
            self.dma(cd[:], dr['cdft'][:, :], (), ['cd'], 'cd')
            Xs = [ph.sb([128, 128, 128], BF16, 'X') for _ in range(2)]
            Z = ph.sb([128, 128, 256], BF16, 'Z')
            E = ph.sb([128, 128 * 3 * CP], BF16, 'E')
            Y = ph.sb([128, 2, NPT * TQ], BF16, 'Y')
            fo = [ph.sb([128, TQ], BF16, 'fo') for _ in range(2)]
            pb = [0]

            def nb():
                b_ = pb[0] % 8
                pb[0] += 1
                return b_
            foi = 0
            for s in range(3):
                NB_ = SRC_S[s] // 128
                C_ = CP if s == 0 else 128
                ntok = C_ * NB_
                w1 = w1p if s == 0 else w1s
                if s == 0:
                    for q4 in range(4):
                        self.dma(E[:, q4 * 32 * 3 * CP:(q4 + 1) * 32 * 3 * CP],
                                 dr['ep'][:, q4 * 32:(q4 + 1) * 32, :].rearrange('a d c -> a (d c)'), (), [('E', q4)], ('E', q4))
                elif s == 1:
                    for q4 in range(4):
                        self.dma(E[:, q4 * 8 * 384:(q4 + 1) * 8 * 384],
                                 dr['esm'][:, q4 * 8:(q4 + 1) * 8, :].rearrange('a d c -> a (d c)'), (), [('E', q4)], ('E', q4))
                Ev = E[:, 0:NB_ * 3 * C_].rearrange('a (d c) -> a d c', c=3 * C_)
                for g in range(4):
                    it_ = s * 4 + g
                    X = Xs[it_ % 2]
                    xkey = ('X', it_ % 2)
                    if it_ == 0:
                        self.dma(X[0:NB_, :, :], dr['f%d' % s][g].rearrange('(b a) c -> b a c', a=128), (), [xkey], xkey)
                    if it_ + 1 < 12:
                        s2_, g2_ = (it_ + 1) // 4, (it_ + 1) % 4
                        nb2_ = SRC_S[s2_] // 128
                        self.dma(Xs[(it_ + 1) % 2][0:nb2_, :, :], dr['f%d' % s2_][g2_].rearrange('(b a) c -> b a c', a=128), (),
                                 [('X', (it_ + 1) % 2)], ('X', (it_ + 1) % 2))
                    nper = 512 // (2 * NB_)
                    for c0 in range(0, 128, nper):
                        bank = nb()
                        for cc in range(nper):
                            ch = c0 + cc
                            self.mm(self.ps[bank][:, cc * 2 * NB_:(cc + 1) * 2 * NB_], X[0:NB_, :, ch], w1[0:NB_, 0:2 * NB_], True, True,
                                    [xkey, 'w1p', 'w1s'], [('ps', bank)])
                        self.cp('act' if (c0 // nper) % 2 else 'dve', Z[:, c0:c0 + nper, 0:2 * NB_],
                                self.ps[bank][:].rearrange('p (c x) -> p c x', x=2 * NB_), [('ps', bank)], ['Z'])
                    nd = 512 // (2 * C_)
                    for d0 in range(0, NB_, nd):
                        bank = nb()
                        ndd = min(nd, NB_ - d0)
                        for dd in range(ndd):
                            d_ = d0 + dd
                            o = self.ps[bank][:, dd * 2 * C_:(dd + 1) * 2 * C_]
                            self.mm(o, Z[:, :, d_], Ev[:, d_, C_:3 * C_], True, False, ['Z'] + [('E', q_) for q_ in range(4)], [('ps', bank)])
                            self.mm(o, Z[:, :, NB_ + d_], Ev[:, d_, 0:2 * C_], False, True, ['Z'] + [('E', q_) for q_ in range(4)], [('ps', bank)])
                        src_ = self.ps[bank][:, 0:ndd * 2 * C_].rearrange('p (d r c) -> p r c d', r=2, c=C_)
                        dst_ = Y[:, :, 0:ntok].rearrange('p r (c d) -> p r c d', d=NB_)[:, :, :, d0:d0 + ndd]
                        self.cp('act' if (d0 // nd) % 2 else 'dve', dst_, src_, [('ps', bank)], ['Y'])
                    for tq in range(ntok // TQ):
                        bank = nb()
                        self.mm(self.ps[bank][:], cd[:, 0:128], Y[:, 0, tq * TQ:(tq + 1) * TQ], True, False, ['Y', 'cd'], [('ps', bank)])
                        self.mm(self.ps[bank][:], cd[:, 128:256], Y[:, 1, tq * TQ:(tq + 1) * TQ], False, True, ['Y', 'cd'], [('ps', bank)])
                        fb = foi % 2; foi += 1
                        self.cp('act' if tq % 2 else 'dve', fo[fb][:], self.ps[bank][:], [('ps', bank)], [('fo', fb)])
                        t = SEQ_T0[s] + tq
                        self.dma(dr['foT'][g, :, t * TQ:(t + 1) * TQ], fo[fb][:], [('fo', fb)], [], ('fo', fb))

    def xa_rms(self, ph, R, XA, X, xk, b, l, nb):
        self.rms(ph, R, X, xk, XA['hT'][b], ('xhT', b), 8, TQ, D, nb())

    def xa_qproj(self, ph, XA, b, l, nb, ocs):
        hT, qx = XA['hT'][b], XA['qx'][b]
        hk, qk = ('xhT', b), ('qx', b, 0)
        Wq = XA['Wq'][l]
        kq = 'Wq%d' % l
        for oc in ocs:
            bank = nb()
            for kc in range(8):
                self.mm(self.ps[bank][:], Wq[:, kc, oc * 128:(oc + 1) * 128], hT[:, kc, :], kc == 0, kc == 7, [hk, kq], [('ps', bank)])
            self.cp('act' if oc % 2 else 'dve', qx[:, oc, :], self.ps[bank][:], [('ps', bank)], [('qx', b, oc)])

    def xa_back(self, ph, R, XA, X, xk, b, l, s, nb, filler=None):
        qx, ox, ptx, rdx = XA['qx'][b], XA['ox'], XA['pt'], XA['rd']
        Wo, mk, mv = XA['Wo'][l], XA['mk'][l][s], XA['mv'][l][s]
        ko, kmk, kmv = 'Wo%d' % l, 'mk', 'mv'

        def st1(hd):
            for kb in range(2):
                bank = nb()
                for dc in range(2):
                    self.mm(self.ps[bank][:], mk[:, hd * 2 + dc, kb * 128:(kb + 1) * 128], qx[:, hd * 2 + dc, :], dc == 0, dc == 1,
                            [('qx', b, hd * 2 + dc), kmk], [('ps', bank)])
                self.act(ptx[hd % 2][kb][:], self.ps[bank][:], AF.Exp, [('ps', bank)], [('ptx', hd % 2, kb)], scale=1.0 / 16.0)

        def st2(hd):
            p = ptx[hd % 2]
            bank = nb()
            for kb in range(2):
                self.mm(self.ps[bank][:], self.onesb[:], p[kb][:], kb == 0, kb == 1, [('ptx', hd % 2, kb), 'onesb'], [('ps', bank)])
            self.act(rdx[hd % 2][:], self.ps[bank][:], AF.Ln, [('ps', bank)], [('rdx', hd % 2)])
            self.act(rdx[hd % 2][:], rdx[hd % 2][:], AF.Exp, [('rdx', hd % 2)], [('rdx', hd % 2)], scale=-1.0)
            for dv in range(2):
                bank = nb()
                for kb in range(2):
                    self.mm(self.ps[bank][:], mv[:, kb, hd * 256 + dv * 128:hd * 256 + (dv + 1) * 128], p[kb][:], kb == 0, kb == 1,
                            [('ptx', hd % 2, kb), kmv], [('ps', bank)])
                self.tt('dve', ox[:, hd * 2 + dv, :], self.ps[bank][:], rdx[hd % 2][:], OP.mult, [('ps', bank), ('rdx', hd % 2)], [('ox', hd * 2 + dv)])
        st1(0)
        for hd in range(4):
            if hd + 1 < 4:
                st1(hd + 1)
            if filler is not None:
                filler(hd)
            st2(hd)
        for oc in range(8):
            bank = nb()
            for kc in range(8):
                self.mm(self.ps[bank][:], Wo[:, kc, oc * 128:(oc + 1) * 128], ox[:, kc, :], kc == 0, kc == 7, [('ox', kc), ko], [('ps', bank)])
            self.tt('dve', X[:, oc, :], self.ps[bank][:], X[:, oc, :], OP.add, [('ps', bank), xk], [xk])

    def xattn_tiles(self, ph, layers):
        dr = self.dram
        XA = dict(hT=[ph.sb([128, 8, TQ], BF16, 'xhT') for _ in range(2)], qx=[ph.sb([128, 8, TQ], BF16, 'qx') for _ in range(2)],
                  ox=ph.sb([128, 8, TQ], BF16, 'ox'),
                  pt=[[ph.sb([128, TQ], BF16, 'ptx') for _ in range(2)] for _ in range(2)],
                  rd=[ph.sb([128, TQ], F32, 'rdx') for _ in range(2)], Wq={}, Wo={}, mk={}, mv={})
        for l in layers:
            XA['Wq'][l] = self.load_w(ph, 'Wq%d' % l, [128, 8, D])
            XA['Wo'][l] = self.load_w(ph, 'Wo%d' % l, [128, 8, D])
            XA['mk'][l] = []; XA['mv'][l] = []
            for s in range(3):
                mk = ph.sb([128, 8, 256], BF16, 'mk'); mv = ph.sb([128, 2, D], BF16, 'mv')
                self.dma(mk[:], dr['memKT'][l, s], (), ['mk'], ('mk', s))
                self.dma(mv[:], dr['memV'][l, s], (), ['mv'], ('mv', s))
                XA['mk'][l].append(mk); XA['mv'][l].append(mv)
        return XA

    def ph_mix(self):
        dr = self.dram
        with Phase(self, 'mix') as ph:
            W_out = self.load_w(ph, 'W_out', [128, 8, D])
            xT = [ph.sb([128, 8, TQ], F32, 'xT') for _ in range(2)]
            mt = [ph.sb([128, 8, TQ], BF16, 'mt') for _ in range(2)]
            pbc = [0]

            def nb():
                b_ = pbc[0] % 8
                pbc[0] += 1
                return b_

            def load(t):
                b = t % 2
                sl = slice(t * TQ, (t + 1) * TQ)
                self.dma(xT[b][:], dr['x0T'][:, :, sl].rearrange('c p t -> p c t'), (), [('xT', b)], ('xT', b))
                self.dma(mt[b][:, 0:4, :], dr['aT'][:, :, sl].rearrange('c p t -> p c t'), (), [('mta', b)], ('mta', b))
                self.dma(mt[b][:, 4:8, :], dr['foT'][:, :, sl].rearrange('c p t -> p c t'), (), [('mtf', b)], ('mtf', b))
            load(0)
            for t in range(NT):
                b = t % 2
                if t + 1 < NT:
                    load(t + 1)
                X = xT[b]; xk = ('xT', b)
                for oc in range(8):
                    bank = nb()
                    for kc in range(8):
                        self.mm(self.ps[bank][:], W_out[:, kc, oc * 128:(oc + 1) * 128], mt[b][:, kc, :], kc == 0, kc == 7,
                                [('mta', b) if kc < 4 else ('mtf', b), 'W_out'], [('ps', bank)])
                    self.tt('dve', X[:, oc, :], self.ps[bank][:], X[:, oc, :], OP.add, [('ps', bank), xk], [xk])
                self.dma(dr['x1T'][:, :, t * TQ:(t + 1) * TQ].rearrange('c p t -> p c t'), X[:], [xk], [], xk)

    def ph_xattn(self, l, xin, xout):
        dr = self.dram
        with Phase(self, 'xa%d' % l) as ph:
            R = self.rms_tiles(ph)
            XA = self.xattn_tiles(ph, [l])
            xT = [ph.sb([128, 8, TQ], F32, 'xT') for _ in range(3)]
            pbc = [0]

            def nb():
                b_ = pbc[0] % 8
                pbc[0] += 1
                return b_

            def load(t):
                b = t % 3
                self.dma(xT[b][:], dr[xin][:, :, t * TQ:(t + 1) * TQ].rearrange('c p t -> p c t'), (), [('xT', b)], ('xT', b))
            load(0)
            load(1)
            self.xa_rms(ph, R, XA, xT[0], ('xT', 0), 0, l, nb)
            self.xa_qproj(ph, XA, 0, l, nb, range(8))
            for t in range(NT):
                b = t % 3
                if t + 2 < NT:
                    load(t + 2)
                s = 0 if t < NPT else (1 if t < NPT + NST else 2)
                fill = None
                if t + 1 < NT:
                    self.xa_rms(ph, R, XA, xT[(t + 1) % 3], ('xT', (t + 1) % 3), (t + 1) % 2, l, nb)
                    fill = (lambda hd, t=t: self.xa_qproj(ph, XA, (t + 1) % 2, l, nb, [2 * hd, 2 * hd + 1]))
                self.xa_back(ph, R, XA, xT[b], ('xT', b), t % 2, l, s, nb, filler=fill)
                self.dma(dr[xout][:, :, t * TQ:(t + 1) * TQ].rearrange('c p t -> p c t'), xT[b][:], [('xT', b)], [], ('xT', b))

    def ph_ffn(self, l, half, xin, xres, xout):
        dr = self.dram
        NH = NFF // 2
        final = xout is None
        with Phase(self, 'ffn%d%d' % (l, half)) as ph:
            R = self.rms_tiles(ph)
            W1 = ph.sb([128, 8, NH * 128], BF16, 'W1'); W3 = ph.sb([128, 8, NH * 128], BF16, 'W3'); W2 = ph.sb([128, NH, D], BF16, 'W2')
            c0 = half * NH * 128
            self.dma(W1[:], dr['W1_%d' % l][:, :, c0:c0 + NH * 128], (), ['W1'], 'W1', q='pool')
            self.dma(W3[:], dr['W3_%d' % l][:, :, c0:c0 + NH * 128], (), ['W3'], 'W3', q='pool')
            self.dma(W2[:], dr['W2_%d' % l][:, half * NH:(half + 1) * NH, :], (), ['W2'], 'W2', q='pool')
            xT = [ph.sb([128, 8, TQ], F32, 'xT') for _ in range(2)]
            xr = ph.sb([128, 8, TQ], F32, 'xr') if xres else None
            xo = [ph.sb([128, 8, TQ], F32, 'xo') for _ in range(2 if final else 1)]
            for i_, x_ in enumerate(xo):
                self.memset('pool', x_[:], 0.0, [('xo', i_)])
            hTs = [ph.sb([128, 8, TQ], BF16, 'hT') for _ in range(2)]
            aT = ph.sb([128, NH, TQ], BF16, 'aT')
            sg = [ph.sb([128, TQ], BF16, 'sg') for _ in range(2)]
            if final:
                gf = self.load_vec(ph, 'gfin', dr['g_final'], 8)
                ytm = [ph.sb([128, D], F32, 'ytm') for _ in range(2)]
                sq2, rs2, rstd2 = R['sq'], R['rs'], R['rstd']
            pbc = [0]
            yi = [0]

            def nb():
                b_ = pbc[0] % 8
                pbc[0] += 1
                return b_

            def load(t):
                b = t % 2
                self.dma(xT[b][:], dr[xin][:, :, t * TQ:(t + 1) * TQ].rearrange('c p t -> p c t'), (), [('xT', b)], ('xT', b))

            def front(t):
                b = t % 2
                c0, n = twin(t)
                self.rms(ph, R, xT[b], ('xT', b), hTs[b], ('hT', b), 8, n, D, nb(), c0=c0)

            def stA(t, mid=None):
                hT = hTs[t % 2]; hk = ('hT', t % 2)
                c0, n = twin(t)
                for j in range(NH):
                    if j == 2 and mid is not None:
                        mid()
                    b1 = nb(); b3 = nb()
                    for kc in range(8):
                        self.mm(self.ps[b1][:, 0:n], W1[:, kc, j * 128:(j + 1) * 128], hT[:, kc, c0:c0 + n], kc == 0, kc == 7, [hk, 'W1'], [('ps', b1)])
                    for kc in range(8):
                        self.mm(self.ps[b3][:, 0:n], W3[:, kc, j * 128:(j + 1) * 128], hT[:, kc, c0:c0 + n], kc == 0, kc == 7, [hk, 'W3'], [('ps', b3)])
                    sb_ = j % 2
                    self.act(sg[sb_][:, 0:n], self.ps[b1][:, 0:n], AF.Silu, [('ps', b1)], [('sg', sb_)])
                    self.tt('dve', aT[:, j, 0:n], self.ps[b3][:, 0:n], sg[sb_][:, 0:n], OP.mult, [('ps', b3), ('sg', sb_)], ['aT'])

            def stB(t):
                b = t % 2
                XO = xo[t % len(xo)]; xok = ('xo', t % len(xo))
                XR = xr if xres else xT[b]
                xrk = 'xr' if xres else ('xT', b)
                c0, n = twin(t)
                for oc in range(8):
                    bank = nb()
                    for j in range(NH):
                        self.mm(self.ps[bank][:, 0:n], W2[:, j, oc * 128:(oc + 1) * 128], aT[:, j, 0:n], j == 0, j == NH - 1, ['aT', 'W2'], [('ps', bank)])
                    self.tt('dve', XO[:, oc, c0:c0 + n], self.ps[bank][:, 0:n], XR[:, oc, c0:c0 + n], OP.add, [('ps', bank), xrk], [xok])
                if not final:
                    self.dma(dr[xout][:, :, t * TQ:(t + 1) * TQ].rearrange('c p t -> p c t'), XO[:], [xok], [], xok)

            def stF0(t):
                XO = xo[t % 2]; xok = ('xo', t % 2)
                self.act(sq2[:], XO[:], AF.Square, [xok], ['sq'])

            def stF1(t):
                XO = xo[t % 2]; xok = ('xo', t % 2)
                sq, rs, rstd = sq2, rs2, rstd2
                bank = nb()
                for c in range(8):
                    self.mm(self.ps[bank][:], self.onesb[:], sq[:, c, :], c == 0, c == 7, ['sq', 'onesb'], [('ps', bank)])
                self.act(rs[:], self.ps[bank][:], AF.Ln, [('ps', bank)], ['rs'], bias=R['eps'][:, 0:1], scale=1.0 / D)
                self.act(self.ps[bank][:], rs[:], AF.Exp, ['rs'], [('ps', bank)], scale=-0.5)
                for c in range(8):
                    self.stt(XO[:, c, :], XO[:, c, :], gf[:, c:c + 1], self.ps[bank][:], OP.mult, OP.mult, [xok, ('ps', bank), 'gfin'], [xok])

            def stF2(t):
                XO = xo[t % 2]; xok = ('xo', t % 2)
                for sub in range(4):
                    yb = yi[0] % 2; yi[0] += 1
                    for hh in range(2):
                        bank = nb()
                        for c4 in range(4):
                            c = hh * 4 + c4
                            self.tr(self.ps[bank][:, c4 * 128:(c4 + 1) * 128], XO[:, c, sub * 128:(sub + 1) * 128], self.identf[:],
                                    [xok, 'identf'], [('ps', bank)])
                        self.cp('act' if hh else 'dve', ytm[yb][:, hh * 512:(hh + 1) * 512], self.ps[bank][:], [('ps', bank)], [('ytm', yb)])
                    r0 = t * TQ + sub * 128
                    self.dma(dr['y'][r0:r0 + 128, :], ytm[yb][:], [('ytm', yb)], [], ('ytm', yb))

            load(0)
            front(0)
            for t in range(NT):
                if t + 1 < NT:
                    load(t + 1)
                if xres:
                    self.dma(xr[:], dr[xres][:, :, t * TQ:(t + 1) * TQ].rearrange('c p t -> p c t'), (), ['xr'], 'xr')
                stA(t, mid=((lambda t=t: stF1(t - 1)) if (final and t >= 1) else None))
                if t + 1 < NT:
                    front(t + 1)
                if final and t >= 1:
                    stF2(t - 1)
                stB(t)
                if final:
                    stF0(t)
            if final:
                stF1(NT - 1)
                stF2(NT - 1)

    def ph_glu(self):
        dr = self.dram
        with Phase(self, 'glu') as ph:
            R = self.rms_tiles(ph)
            Wp1 = self.load_w(ph, 'W_pw1', [128, 8, 2048])
            bp1 = self.load_vec(ph, 'bp1', dr['b_pw1'], 16)
            xT = [ph.sb([128, 8, TQ], F32, 'xT') for _ in range(3)]
            mk_ = [ph.sb([128, TQ], F32, 'mk') for _ in range(3)]
            U = [ph.sb([128, 8, TQ], BF16, 'U') for _ in range(2)]
            hTs = [ph.sb([128, 8, TQ], BF16, 'hT') for _ in range(2)]
            sgt = [ph.sb([128, TQ], F32, 'sgt') for _ in range(2)]
            utmp = [ph.sb([128, TQ], F32, 'utmp') for _ in range(2)]
            pbc = [0]

            def nb():
                b_ = pbc[0] % 8
                pbc[0] += 1
                return b_

            def load(t):
                sl = slice(t * TQ, (t + 1) * TQ)
                self.dma(xT[t % 3][:], dr['x2T'][:, :, sl].rearrange('c p t -> p c t'), (), [('xT', t % 3)], ('xT', t % 3))
                if t in (0, NPT - 1):
                    self.dma(mk_[t % 3][:], dr['mask'][:, sl], (), [('mk', t % 3)], ('mk', t % 3))

            def front(t):
                self.rms(ph, R, xT[t % 3], ('xT', t % 3), hTs[t % 2], ('hT', t % 2), 8, TQ, D, nb())
            load(0)
            load(1)
            front(0)
            for t in range(NT):
                b = t % 2
                uk = ('U', b)
                hT = hTs[b]; hk = ('hT', b)
                if t + 2 < NT:
                    load(t + 2)
                if t + 1 < NT:
                    front(t + 1)
                for oc in range(8):
                    ba = nb(); bg = nb()
                    for kc in range(8):
                        self.mm(self.ps[ba][:], Wp1[:, kc, oc * 128:(oc + 1) * 128], hT[:, kc, :], kc == 0, kc == 7, [hk, 'W_pw1'], [('ps', ba)])
                    for kc in range(8):
                        self.mm(self.ps[bg][:], Wp1[:, kc, 1024 + oc * 128:1024 + (oc + 1) * 128], hT[:, kc, :], kc == 0, kc == 7,
                                [hk, 'W_pw1'], [('ps', bg)])
                    i2 = oc % 2
                    self.act(sgt[i2][:], self.ps[bg][:], AF.Sigmoid, [('ps', bg), 'bp1'], [('sgt', i2)], bias=bp1[:, 8 + oc:9 + oc])
                    if t in (0, NPT - 1):
                        self.stt(utmp[i2][:], self.ps[ba][:], bp1[:, oc:oc + 1], sgt[i2][:], OP.add, OP.mult, [('ps', ba), ('sgt', i2), 'bp1'], [('utmp', i2)])
                        self.tt('pool', U[b][:, oc, :], utmp[i2][:], mk_[t % 3][:], OP.mult, [('utmp', i2), ('mk', t % 3)], [uk])
                    else:
                        self.stt(U[b][:, oc, :], self.ps[ba][:], bp1[:, oc:oc + 1], sgt[i2][:], OP.add, OP.mult, [('ps', ba), ('sgt', i2), 'bp1'], [uk])
                self.dma(dr['uT'][:, :, t * TQ:(t + 1) * TQ].rearrange('c p t -> p c t'), U[b][:], [uk], [], uk)

    def ph_conv2(self):
        dr = self.dram
        with Phase(self, 'conv') as ph:
            eps_t = ph.sb([128, 1], F32, 'eps')
            self.memset('dve', eps_t[:], EPS, ['epsc'])
            Wp2 = self.load_w(ph, 'W_pw2', [128, 8, D])
            bdw = self.load_vec(ph, 'bdw', dr['b_dw'], 8)
            gln = self.load_vec(ph, 'gln', dr['g_ln'], 8)
            bln = self.load_vec(ph, 'bln', dr['b_ln'], 8)
            bp2 = self.load_vec(ph, 'bp2', dr['b_pw2'], 8)
            wsel = ph.sb([128, 8, 4, 8], F32, 'wsel')
            self.dma(wsel[:].rearrange('p c b g -> p (c b g)'), dr['w_dw'][:, :], (), ['wsel'], 'wsel')
            p4 = ph.sb([128, 32], F32, 'p4')
            self.dma(p4[:], dr['p4'][:, :], (), ['p4'], 'p4')
            W4 = ph.sb([128, 8, 4, 8, 32], BF16, 'W4')
            ii = 0
            for c in range(8):
                for cb in range(4):
                    self.tt('dve' if ii % 2 else 'pool', W4[:, c, cb, :, :], p4[:, None, :].broadcast_to([128, 8, 32]),
                            wsel[:, c, cb, :, None].broadcast_to([128, 8, 32]), OP.mult, ['p4', 'wsel'], ['W4'])
                    ii += 1
            XT = ph.sb([128, 8, TQ], F32, 'xT'); xk = 'xT'
            UW = TQ + 28
            ur = [ph.sb([128, 8, 4, UW], BF16, 'ur') for _ in range(2)]
            cvs = [ph.sb([128, 8, TQ], F32, 'cv') for _ in range(2)]
            cvbs = [ph.sb([128, 8, TQ], BF16, 'cvb') for _ in range(2)]
            sqbs = [ph.sb([128, 8, TQ], BF16, 'sqb') for _ in range(2)]
            mean = ph.sb([128, TQ], F32, 'mean'); msq = ph.sb([128, TQ], F32, 'msq'); var = ph.sb([128, TQ], F32, 'var')
            sd = ph.sb([128, TQ], F32, 'sd'); rstd2 = ph.sb([128, TQ], F32, 'rstd2')
            pbc = [0]

            def nb():
                b_ = pbc[0] % 8
                pbc[0] += 1
                return b_

            def flags(t):
                s = 0 if t < NPT else (1 if t < NPT + NST else 2)
                return (t == SEQ_T0[s]), (t == SEQ_T0[s] + SEQ_NT[s] - 1)

            def load_u(t):
                b = t % 2
                first, last = flags(t)
                if first:
                    self.memset('pool', ur[b][:, :, :, 0:16], 0.0, [('ur', b, r_) for r_ in range(4)])
                if last:
                    self.memset('pool', ur[b][:, :, :, 524:UW], 0.0, [('ur', b, r_) for r_ in range(4)])
                for r in range(4):
                    lo = (15 - r) if first else 0
                    hi = (TQ + 15 - r) if last else UW
                    g0 = t * TQ - 15 + r
                    self.dma(ur[b][32 * r:32 * r + 32, :, :, lo:hi],
                             dr['uT'][:, :, g0 + lo:g0 + hi].rearrange('c (b p) t -> p c b t', p=32), (), [('ur', b, r)], ('ur', b, r),
                             q=('sp' if r % 2 == 0 else 'act'))

            def stA(t):
                b = t % 2
                U = ur[b]; uks = [('ur', b, r_) for r_ in range(4)]
                cv, cvb, sqb = cvs[b], cvbs[b], sqbs[b]
                for oc in range(8):
                    bank = nb()
                    for G in range(8):
                        for cb in range(4):
                            self.mm(self.ps[bank][32 * cb:32 * cb + 32, :], W4[:, oc, cb, G, :], U[:, oc, cb, 4 * G:4 * G + TQ], G == 0, G == 7,
                                    uks + ['W4'], [('ps', bank)], tp=((0, 96) if cb == 3 else None))
                    self.act(cv[:, oc, :], self.ps[bank][:], AF.Identity, [('ps', bank), 'bdw'], [('cv', b)], bias=bdw[:, oc:oc + 1])
                    self.act(sqb[:, oc, :], self.ps[bank][:], AF.Square, [('ps', bank), 'bdw'], [('sqb', b)], bias=bdw[:, oc:oc + 1])
                    self.cp('pool' if oc % 2 else 'dve', cvb[:, oc, :], cv[:, oc, :], [('cv', b)], [('cvb', b), ('z', b)])

            def stB1a(t):
                b = t % 2
                cv, cvb, sqb = cvs[b], cvbs[b], sqbs[b]
                ck, cbk, sk = ('cv', b), ('cvb', b), ('sqb', b)
                self.dma(XT[:], dr['x2T'][:, :, t * TQ:(t + 1) * TQ].rearrange('c p t -> p c t'), (), [xk], xk)
                b1 = nb(); b2 = nb()
                for c in range(8):
                    self.mm(self.ps[b1][:], self.onesb[:], cvb[:, c, :], c == 0, c == 7, [cbk, 'onesb'], [('ps', b1)])
                for c in range(8):
                    self.mm(self.ps[b2][:], self.onesb[:], sqb[:, c, :], c == 0, c == 7, [sk, 'onesb'], [('ps', b2)])
                self.act(mean[:], self.ps[b1][:], AF.Copy, [('ps', b1)], ['mean'], scale=1.0 / D)
                self.tt('dve', msq[:], mean[:], mean[:], OP.mult, ['mean'], ['msq'])
                self.stt(var[:], self.ps[b2][:], 1.0 / D, msq[:], OP.mult, OP.subtract, [('ps', b2), 'msq'], ['var'])
                self.act(sd[:], var[:], AF.Ln, ['var'], ['sd'], bias=eps_t[:, 0:1])
                self.act(rstd2[:], sd[:], AF.Exp, ['sd'], ['rstd2'], scale=-0.5)
                for oc in range(8):
                    e_ = 'pool' if oc % 2 else 'dve'
                    ckc = ('cvc', b, oc)
                    self.tt(e_, cv[:, oc, :], cv[:, oc, :], mean[:], OP.subtract, [ck, 'mean'], [ckc])
                    self.tt(e_, cv[:, oc, :], cv[:, oc, :], rstd2[:], OP.mult, [ckc, 'rstd2'], [ckc])

            def stB1b(t):
                b = t % 2
                cv, cvb = cvs[b], cvbs[b]
                cbk = ('cvb', b)
                for oc in range(8):
                    ckc = ('cvc', b, oc)
                    self.act(cvb[:, oc, :], cv[:, oc, :], AF.Silu, [ckc, 'gln', 'bln'], [('z', b), cbk], bias=bln[:, oc:oc + 1], scale=gln[:, oc:oc + 1])

            def stB2(t):
                b = t % 2
                cvb = cvbs[b]
                for oc in range(8):
                    bank = nb()
                    for kc in range(8):
                        self.mm(self.ps[bank][:], Wp2[:, kc, oc * 128:(oc + 1) * 128], cvb[:, kc, :], kc == 0, kc == 7, [('z', b), 'W_pw2'], [('ps', bank)])
                    self.stt(XT[:, oc, :], self.ps[bank][:], bp2[:, oc:oc + 1], XT[:, oc, :], OP.add, OP.add, [('ps', bank), xk, 'bp2'], [xk])
                self.dma(dr['x2bT'][:, :, t * TQ:(t + 1) * TQ].rearrange('c p t -> p c t'), XT[:], [xk], [], xk)

            load_u(0)
            load_u(1)
            stA(0)
            for t in range(NT):
                stB1a(t)
                if t + 1 < NT:
                    stA(t + 1)
                if t + 2 < NT:
                    load_u(t + 2)
                stB1b(t)
                stB2(t)


def _bf(a):
    return np.asarray(a, dtype=np.float32).astype(ml_dtypes.bfloat16)


def _const_tables():
    c = {}
    c['identf'] = np.eye(128, dtype=np.float32)
    c['p4'] = np.ascontiguousarray(np.tile(np.eye(32, dtype=np.float32), (4, 1)))
    for nm, nbk, S_ in (('w1p', 128, SP), ('w1s', 32, SS)):
        b = np.arange(nbk)[:, None].astype(np.float64); d = np.arange(nbk)[None, :].astype(np.float64)
        th = 2 * np.pi * ((b * d) % nbk) / nbk
        c[nm] = _bf(np.concatenate([np.cos(th), -np.sin(th)], axis=1) / np.sqrt(S_))
    ch = np.arange(128)[:, None].astype(np.float64); m = np.arange(128)[None, :].astype(np.float64)
    ph_ = 2 * np.pi * ((ch * m) % 128) / 128
    c['cdft'] = _bf(np.concatenate([np.cos(ph_), np.sin(ph_)], axis=1) / np.sqrt(128.0))
    a = np.arange(128, dtype=np.int64)[:, None, None]; d = np.arange(32, dtype=np.int64)[None, :, None]; cc = np.arange(128, dtype=np.int64)[None, None, :]
    k = 32 * cc + d
    th = 2 * np.pi * ((k * a) % SS).astype(np.float64) / SS
    c['esm'] = _bf(np.concatenate([np.sin(th), np.cos(th), -np.sin(th)], axis=2))
    return c


def _ep_table(j):
    a = np.arange(128, dtype=np.int64)[:, None, None]; d = np.arange(128, dtype=np.int64)[None, :, None]
    cc = ((32 * j - 2 + np.arange(CP, dtype=np.int64)) % 128)[None, None, :]
    k = 128 * cc + d
    th = 2 * np.pi * ((k * a) % SP).astype(np.float64) / SP
    return _bf(np.concatenate([np.sin(th), np.cos(th), -np.sin(th)], axis=2))


def _rope_tab(pos):
    half = 16
    inv = (10000.0 ** (-np.arange(half, dtype=np.float32) / half)).astype(np.float32)
    ang = pos.astype(np.float32)[None, :] * inv[:, None]
    cos = np.cos(ang).astype(np.float32); sin = np.sin(ang).astype(np.float32)
    cos2 = np.concatenate([cos, cos], axis=0)
    sinp = np.concatenate([-sin, sin], axis=0)
    return np.stack([cos2, sinp], axis=0)


_NC_CACHE = {}


def _prep_inputs(inp):
    f = lambda k: np.asarray(inp[k], dtype=np.float32)
    shared = {}
    w_in = f('ab_w_in')[0]
    kr = w_in[:, 640:672]
    shared['w_in'] = np.ascontiguousarray(np.concatenate([w_in[:, 0:672], kr[:, 16:32], kr[:, 0:16], w_in[:, 672:1184]], axis=1))
    wuq = f('mla_w_uq')[0].reshape(384, H, 96)
    A = wuq
    B = np.concatenate([wuq[:, :, 0:64], wuq[:, :, 80:96], wuq[:, :, 64:80]], axis=2)
    shared['w_uq'] = np.ascontiguousarray(np.concatenate([A, B], axis=2).reshape(384, H * 192))
    wukv = f('mla_w_ukv')[0].reshape(256, H, 128)
    shared['w_ukv'] = np.ascontiguousarray(np.concatenate([wukv[:, :, 0:64].reshape(256, 512), wukv[:, :, 64:128].reshape(256, 512)], axis=1))
    shared['w_out'] = f('ab_w_out')[0]
    shared['pw1'] = f('conv_w_pw1')[0]; shared['pw2'] = f('conv_w_pw2')[0]
    for l in range(2):
        shared['wq%d' % l] = f('xa_wq')[l]; shared['wk%d' % l] = f('xa_wk')[l]
        shared['wv%d' % l] = f('xa_wv')[l]; shared['wo%d' % l] = f('xa_wo')[l]
        shared['w1_%d' % l] = f('ffn_w1')[l]; shared['w3_%d' % l] = f('ffn_w3')[l]; shared['w2_%d' % l] = f('ffn_w2')[l]
    shared['g_mix'] = f('g_mix'); shared['g_xq'] = f('g_xq'); shared['g_xkv'] = f('g_xkv'); shared['g_ffn'] = f('g_ffn')
    shared['g_q'] = f('mla_g_q')[0]; shared['g_kv'] = f('mla_g_kv')[0]
    shared['b_pw1'] = f('conv_b_pw1')[0]; w32 = np.concatenate([f('conv_w_dw')[0], np.zeros((1, D), np.float32)], axis=0)
    shared['w_dw'] = np.ascontiguousarray(w32.reshape(8, 4, 8, 4, 32).transpose(1, 4, 2, 3, 0).reshape(128, 256))
    shared['b_dw'] = f('conv_b_dw')[0]
    shared['g_ln'] = f('conv_g_ln')[0]; shared['b_ln'] = f('conv_b_ln')[0]; shared['b_pw2'] = f('conv_b_pw2')[0]
    shared['g_final'] = f('g_final')
    shared.update(_const_tables())
    shared['ropek'] = _rope_tab(np.arange(SP))
    xp_all = f('x_prompt'); xs_all = f('x_sample'); mp = f('mem_prompt'); ms = f('mem_sample')
    maps = []
    for c in range(8):
        seq, j = c // 4, c % 4
        pos = 4096 * j - HALO + np.arange(NPT * TQ)
        idx = pos % SP
        m = dict(shared)
        m['xo'] = np.ascontiguousarray(np.concatenate([xp_all[seq][idx], xs_all[2 * c], xs_all[2 * c + 1]], axis=0))
        m['xp'] = xp_all[seq]
        m['mem'] = np.ascontiguousarray(np.stack([mp[seq], ms[2 * c], ms[2 * c + 1]], axis=0))
        posq = np.concatenate([idx, np.arange(SS), np.arange(SS)])
        m['ropeq'] = _rope_tab(posq)
        valid = ((pos >= 0) & (pos < SP)).astype(np.float32)
        mk = np.concatenate([valid, np.ones(2 * SS, np.float32)])
        m['mask'] = np.ascontiguousarray(np.broadcast_to(mk[None, :], (128, T_OWN)))
        m['ep'] = _ep_table(j)
        maps.append(m)
    return maps


def _assemble(results):
    yp = np.zeros((2, SP, D), np.float32)
    ys = np.zeros((16, SS, D), np.float32)
    for c in range(8):
        y = results[c]['y']
        seq, j = c // 4, c % 4
        yp[seq, 4096 * j:4096 * (j + 1)] = y[HALO:HALO + 4096]
        ys[2 * c] = y[NPT * TQ:NPT * TQ + SS]
        ys[2 * c + 1] = y[NPT * TQ + SS:NPT * TQ + 2 * SS]
    return yp, ys


def kernel(**inputs):
    if 'nc' not in _NC_CACHE:
        _NC_CACHE['nc'] = Kern().build()
    nc = _NC_CACHE['nc']
    maps = _prep_inputs(inputs)
    res = run_bass_kernel_spmd(nc, maps, core_ids=list(range(8)))
    return _assemble(res.results)
```
